# Optimizing a Trainium2 kernel written in Bass

```python
import jax, jax.numpy as jnp
from jax import lax
import numpy as np

D_MODEL = 4096
BATCH = 2
SEQ = 4096
DEPTH = 1

N_BRANCHES = 2
POOL_WIDTH = D_MODEL // 2
CONV_WIDTH = D_MODEL // 2
POOL_WINDOWS = (2, 4, 8, 16)
N_POOL_GROUPS = len(POOL_WINDOWS)
POOL_GROUP_DIM = POOL_WIDTH // N_POOL_GROUPS
CONV_K = 3
NORM_EPS = 1e-6
SPLIT_SIZES = (POOL_WIDTH, POOL_WIDTH, CONV_WIDTH, CONV_WIDTH, CONV_WIDTH, CONV_WIDTH,
               N_BRANCHES * D_MODEL)
PROJ_WIDTH = sum(SPLIT_SIZES)
SPLIT_POINTS = tuple(int(v) for v in np.cumsum(SPLIT_SIZES)[:-1])

kernel_name = "hybrid_pool_shortconv_gated_merge"


def rmsnorm(x, w):
    xf = x.astype(jnp.float32)
    y = xf * lax.rsqrt(jnp.mean(xf * xf, axis=-1, keepdims=True) + NORM_EPS)
    return (y * w.astype(jnp.float32)).astype(x.dtype)


def causal_multiscale_pool(u, pool_w, pool_scale):
    b, s, _ = u.shape
    ug = u.reshape(b, s, N_POOL_GROUPS, POOL_GROUP_DIM).astype(jnp.float32)
    cs = lax.cumsum(ug, axis=1)
    cs_pad = jnp.concatenate([jnp.zeros((b, 1, N_POOL_GROUPS, POOL_GROUP_DIM), jnp.float32), cs], axis=1)
    t1 = jnp.arange(1, s + 1, dtype=jnp.float32)
    pooled = []
    for g, w in enumerate(POOL_WINDOWS):
        upper = cs_pad[:, 1:, g]
        lower = jnp.concatenate([jnp.zeros((b, w - 1, POOL_GROUP_DIM), jnp.float32),
                                 cs_pad[:, : s + 1 - w, g]], axis=1)
        count = jnp.minimum(t1, jnp.float32(w))[None, :, None]
        pooled.append((upper - lower) / count - ug[:, :, g])
    pooled = jnp.stack(pooled, axis=2).astype(u.dtype)
    mixed = jnp.einsum('bsgc,gcd->bsgd', pooled, pool_w)
    return mixed.reshape(b, s, POOL_WIDTH) * pool_scale


def causal_gated_shortconv(u, b_gate, c_gate, conv_w, conv_b):
    s = u.shape[1]
    v = c_gate * u
    vpad = jnp.pad(v, ((0, 0), (CONV_K - 1, 0), (0, 0)))
    y = conv_b + sum(conv_w[k] * vpad[:, k:k + s] for k in range(CONV_K))
    return b_gate * y


def setup_inputs(seed: int = 0) -> dict:
    key = jax.random.key(seed)
    ks = jax.random.split(key, 11)
    d = D_MODEL
    x = jax.random.normal(ks[0], (BATCH, SEQ, d), jnp.float32)
    norm_w = 1.0 + 0.02 * jax.random.normal(ks[1], (DEPTH, d), jnp.float32)
    w_in = jax.random.normal(ks[2], (DEPTH, d, PROJ_WIDTH), jnp.float32) * d ** -0.5
    pool_w = jax.random.normal(ks[3], (DEPTH, N_POOL_GROUPS, POOL_GROUP_DIM, POOL_GROUP_DIM), jnp.float32) * POOL_GROUP_DIM ** -0.5
    pool_scale = 1.0 + 0.1 * jax.random.normal(ks[4], (DEPTH, POOL_WIDTH), jnp.float32)
    conv_w = jax.random.normal(ks[5], (DEPTH, CONV_K, CONV_WIDTH), jnp.float32) * CONV_K ** -0.5
    conv_b = 0.02 * jax.random.normal(ks[6], (DEPTH, CONV_WIDTH), jnp.float32)
    gate_b = 0.02 * jax.random.normal(ks[7], (DEPTH, N_BRANCHES, d), jnp.float32)
    w_branch = jax.random.normal(ks[8], (DEPTH, N_BRANCHES, POOL_WIDTH, d), jnp.float32) * POOL_WIDTH ** -0.5
    w_out = jax.random.normal(ks[9], (DEPTH, d, d), jnp.float32) * d ** -0.5
    final_norm_w = 1.0 + 0.02 * jax.random.normal(ks[10], (d,), jnp.float32)
    return {"x": x, "norm_w": norm_w, "w_in": w_in, "pool_w": pool_w,
            "pool_scale": pool_scale, "conv_w": conv_w, "conv_b": conv_b,
            "gate_b": gate_b, "w_branch": w_branch, "w_out": w_out,
            "final_norm_w": final_norm_w}


def reference(x, norm_w, w_in, pool_w, pool_scale, conv_w, conv_b, gate_b, w_branch, w_out, final_norm_w):
    b, s, d = x.shape
    for l in range(DEPTH):
        h = rmsnorm(x, norm_w[l])
        proj = jnp.einsum('bsd,de->bse', h, w_in[l])
        u_p, z_p, u_c, b_c, c_c, z_c, g_logit = jnp.split(proj, SPLIT_POINTS, axis=-1)
        y_pool = causal_multiscale_pool(u_p, pool_w[l], pool_scale[l]) * jax.nn.silu(z_p)
        y_conv = causal_gated_shortconv(u_c, b_c, c_c, conv_w[l], conv_b[l]) * jax.nn.silu(z_c)
        ys = jnp.stack([y_pool, y_conv], axis=2)
        br = jnp.einsum('bsnc,ncd->bsnd', ys, w_branch[l])
        gates = jax.nn.sigmoid(g_logit.reshape(b, s, N_BRANCHES, d) + gate_b[l])
        merged = jnp.sum(gates * br, axis=2)
        x = x + jnp.einsum('bsd,de->bse', merged, w_out[l])
    return rmsnorm(x, final_norm_w)
```

```python
import contextlib
import numpy as np
import concourse.bass as bass
import concourse.mybir as mybir
from concourse.bass_utils import run_bass_kernel_spmd

F32 = mybir.dt.float32
BF16 = mybir.dt.bfloat16
AF = mybir.ActivationFunctionType
ALU = mybir.AluOpType

NCORES = 8
D = 4096
NCH = D // 128
PWD = 2048
NPC = PWD // 128
T = 512
HALO = 16
TT = T + HALO
HW = TT // 2
NHALF = 2
NSLOT = 5
NXS = 5
NP0 = NXS + 5
EPS = 1e-6
WINDOWS = (2, 4, 8, 16)

PC_NORMW = 0
PC_PSCALE = 32
PC_CW0 = 48
PC_CW1 = 64
PC_CW2 = 80
PC_CB = 96
PC_GB0 = 112
PC_GB1 = 144
PC_FNW = 176
NPRM = 208

COMPUTE = ("pe", "act", "dve", "pool")
ENGS = ("pe", "act", "dve", "sp", "pool")


class Region:
    def __init__(self, name):
        self.name = name
        self.bufs = []


class Buf:
    def __init__(self, name, region=None, lo=0, hi=0):
        self.name = name
        self.w = None
        self.r = []
        self.lo, self.hi = lo, hi
        self.aliases = []
        if region is not None:
            for o in region.bufs:
                if o.lo < hi and lo < o.hi:
                    o.aliases.append(self)
                    self.aliases.append(o)
            region.bufs.append(self)


class Prog:
    def __init__(self):
        self.ops = {e: [] for e in ENGS}
        self.seq = {e: 0 for e in COMPUTE}
        self.waited = {e: {} for e in ENGS}
        self.needed = {e: set() for e in COMPUTE}
        self.dma_cnt = {}

    def op(self, eng, fn, reads=(), writes=(), dma_sem=None):
        deps = {}

        def add(sig, same_ok):
            if sig is None:
                return
            key, val = sig
            if key == eng and not same_ok:
                return
            if deps.get(key, 0) < val:
                deps[key] = val

        wset = []
        for b in writes:
            wset.append(b)
            wset.extend(b.aliases)
        for b in reads:
            add(b.w, True)
            for o in b.aliases:
                add(o.w, True)
        for b in wset:
            add(b.w, eng != "pe")
            for s in b.r:
                add(s, False)
        waits = []
        wd = self.waited[eng]
        for key, val in deps.items():
            if wd.get(key, 0) >= val:
                continue
            wd[key] = val
            waits.append((key, val))
            if key in COMPUTE:
                self.needed[key].add(val)
        if dma_sem is None:
            self.seq[eng] += 1
            sig = (eng, self.seq[eng])
        else:
            self.dma_cnt[dma_sem] = self.dma_cnt.get(dma_sem, 0) + 16
            sig = (dma_sem, self.dma_cnt[dma_sem])
        self.ops[eng].append((waits, fn, sig))
        for b in reads:
            b.r.append(sig)
        for b in wset:
            b.w = sig
            b.r = []
        return sig

    def emit(self, block, sems, final_waits):
        ranks = {}
        for e in COMPUTE:
            ranks[e] = {v: i + 1 for i, v in enumerate(sorted(self.needed[e]))}

        def tr(key, val):
            if key in COMPUTE:
                return ranks[key][val]
            return val

        def make_body(eng):
            ops = self.ops[eng]

            def body(e):
                for waits, fn, sig in ops:
                    for key, val in waits:
                        e.wait_ge(sems[key], tr(key, val))
                    ins = fn(e)
                    if sig[0] in COMPUTE:
                        if sig[1] in ranks[eng]:
                            ins.then_inc(sems[eng], 1)
                    else:
                        ins.then_inc(sems[sig[0]], 16)
                if eng == "sp":
                    for key, val in final_waits:
                        e.wait_ge(sems[key], tr(key, val))
            return body

        block.tensor(make_body("pe"))
        block.scalar(make_body("act"))
        block.vector(make_body("dve"))
        block.sync(make_body("sp"))
        block.gpsimd(make_body("pool"))


def build_program():
    nc = bass.Bass("TRN2", target_bir_lowering=False)
    xT_d = nc.dram_tensor("xT", [NHALF, 128, NCH * TT], F32, kind="ExternalInput").ap()
    xTm_d = nc.dram_tensor("xTm", [NHALF, 128, NCH * T], F32, kind="ExternalInput").ap()
    prm_d = nc.dram_tensor("prm", [128, NPRM], F32, kind="ExternalInput").ap()
    pos_d = nc.dram_tensor("pos", [128, NHALF * 16], F32, kind="ExternalInput").ap()
    win_d = nc.dram_tensor("win", [160, 128, 4096], F32, kind="ExternalInput").ap()
    pw_d = nc.dram_tensor("pw", [4, 128, 2048], F32, kind="ExternalInput").ap()
    wbr_d = nc.dram_tensor("wbr", [2, 32, 128, 2048], F32, kind="ExternalInput").ap()
    wo_d = nc.dram_tensor("wo", [32, 128, 4096], F32, kind="ExternalInput").ap()
    out_d = nc.dram_tensor("outT", [NHALF, 128, NCH * T], F32, kind="ExternalOutput").ap()

    P = Prog()
    es = contextlib.ExitStack()
    with es:
        AM = es.enter_context(nc.sbuf_tensor("AM", [128, 24576], F32))
        Hh = es.enter_context(nc.sbuf_tensor("Hh", [128, 8448], F32))
        P0 = es.enter_context(nc.sbuf_tensor("P0", [128, NP0 * TT], F32))
        S3t = es.enter_context(nc.sbuf_tensor("S3t", [128, 4 * T], F32))
        wring = [es.enter_context(nc.sbuf_tensor(f"wr{i}", [128, 4096], BF16)) for i in range(NSLOT)]
        pooled = es.enter_context(nc.sbuf_tensor("pooled", [128, 4 * T], BF16))
        prm = es.enter_context(nc.sbuf_tensor("prm_sb", [128, NPRM], F32))
        pos = es.enter_context(nc.sbuf_tensor("pos_sb", [128, NHALF * 16], F32))
        invc = es.enter_context(nc.sbuf_tensor("invc", [128, NHALF * 4 * 16], F32))
        ones = es.enter_context(nc.sbuf_tensor("ones", [128, 128], F32))
        tmp16 = es.enter_context(nc.sbuf_tensor("tmp16", [128, 16], F32))
        saveU = es.enter_context(nc.sbuf_tensor("saveU", [128, NPC * 16], F32))
        saveV = es.enter_context(nc.sbuf_tensor("saveV", [128, NPC * 16], F32))
        ps = [es.enter_context(nc.psum_tensor(f"ps{i}", [128, 512], F32)) for i in range(8)]

        sem_names = list(COMPUTE) + [f"w{i}" for i in range(NSLOT)] + [f"xl{i}" for i in range(8)] + \
            [f"xs{i}" for i in range(NXS)] + [f"xr{i}" for i in range(8)] + [f"o{i}" for i in range(8)] + \
            ["prm", "pos"]
        sems = {n: es.enter_context(nc.semaphore(n)) for n in sem_names}
        block = es.enter_context(nc.Block())

        rAM, rH, rP0, rS3 = Region("AM"), Region("H"), Region("P0"), Region("S3")

        def f32view(t, blo, n):
            return t[:, blo // 4:blo // 4 + n]

        def bf16view(t, blo, n):
            return t[:, blo // 4:(blo + 2 * n) // 4].bitcast(BF16)

        XN = [f32view(AM, e * 2048, T) for e in range(NCH)]
        b_xn = [Buf(f"xn{e}", rAM, e * 2048, (e + 1) * 2048) for e in range(NCH)]
        YS = [bf16view(AM, j * 1024, T) for j in range(NCH)]
        b_ys = [Buf(f"ys{j}", rAM, j * 1024, (j + 1) * 1024) for j in range(NCH)]
        SCR0 = 32768
        W528 = [f32view(AM, SCR0 + i * 2112, TT) for i in range(6)]
        b_W528 = [Buf(f"W528_{i}", rAM, SCR0 + i * 2112, SCR0 + (i + 1) * 2112) for i in range(6)]
        W512_0 = SCR0 + 6 * 2112
        W512 = [f32view(AM, W512_0 + i * 2048, T) for i in range(8)]
        b_W512 = [Buf(f"W512_{i}", rAM, W512_0 + i * 2048, W512_0 + (i + 1) * 2048) for i in range(8)]
        assert W512_0 + 8 * 2048 <= 65536
        MG0 = 65536
        MG = [bf16view(AM, MG0 + m * 1024, T) for m in range(NCH)]
        b_mg = [Buf(f"mg{m}", rAM, MG0 + m * 1024, MG0 + (m + 1) * 1024) for m in range(NCH)]
        XT0 = [f32view(AM, c * 2112, TT) for c in range(NCH)]
        b_xt0 = [Buf(f"xt0_{c}", rAM, c * 2112, (c + 1) * 2112) for c in range(NCH)]

        hT_all = Hh[:, :].bitcast(BF16)

        def hT(c, lo=0, hi=TT):
            return hT_all[:, c * TT + lo:c * TT + hi]
        b_hT = [Buf(f"hT{c}", rH, c * 1056, (c + 1) * 1056) for c in range(NCH)]

        P0t = [f32view(P0, i * 2112, TT) for i in range(NP0)]
        b_P0 = [Buf(f"P0_{i}", rP0, i * 2112, (i + 1) * 2112) for i in range(NP0)]
        S3 = [f32view(S3t, i * 2048, T) for i in range(4)]
        b_S3 = [Buf(f"S3_{i}", rS3, i * 2048, (i + 1) * 2048) for i in range(4)]

        b_slot = [Buf(f"slot{i}") for i in range(NSLOT)]
        b_pooled = [Buf(f"pooled{i}") for i in range(4)]
        b_prm, b_pos, b_invc, b_ones, b_tmp16 = Buf("prm"), Buf("pos"), Buf("invc"), Buf("ones"), Buf("tmp16")
        b_saveU = [Buf(f"saveU{j}") for j in range(NPC)]
        b_saveV = [Buf(f"saveV{j}") for j in range(NPC)]
        b_bank = [Buf(f"bank{i}") for i in range(8)]

        def pcol(c):
            return prm[:, c:c + 1]

        state = {"bank": 0, "hs": 0, "unit": 0}

        def next_bank():
            i = state["bank"]
            state["bank"] = (i + 1) % 6
            return i

        def next_hs():
            i = state["hs"]
            state["hs"] = (i + 1) % 2
            return 6 + i

        def load_unit(src_ap, ncols):
            s = state["unit"] % NSLOT
            state["unit"] += 1
            dst = wring[s][:, 0:ncols]
            rd = list(b_xt0) if state["unit"] == 1 else []
            P.op("pool", lambda g, dst=dst, src_ap=src_ap: g.dma_start(out=dst, in_=src_ap, max_dma_last_dim=8192),
                 reads=rd, writes=[b_slot[s]], dma_sem=f"w{s}")
            return s

        def mm_unit(s, K, rhs_main, rhs_bufs, halo_rhs=None, woff=0, split=False, nmain=T, nhalo=HALO):
            bk = next_bank()
            hs = next_hs() if halo_rhs is not None else None
            wt = wring[s]
            bank_ap = ps[bk][:, 0:nmain]
            hs_ap = ps[hs][:, 0:nhalo] if hs is not None else None
            if split:
                for k in range(K):
                    def fnk(t, k=k):
                        w_ap = wt[:, woff + k * 128:woff + (k + 1) * 128]
                        ins = t.matmul(bank_ap, w_ap, rhs_main[k], start=(k == 0), stop=(k == K - 1))
                        if hs_ap is not None:
                            ins = t.matmul(hs_ap, w_ap, halo_rhs[k], start=(k == 0), stop=(k == K - 1))
                        return ins
                    P.op("pe", fnk, reads=[b_slot[s], rhs_bufs[k]],
                         writes=[b_bank[bk]] + ([b_bank[hs]] if hs is not None else []))
                return bk, hs

            def fn(t):
                ins = None
                for k in range(K):
                    w_ap = wt[:, woff + k * 128:woff + (k + 1) * 128]
                    ins = t.matmul(bank_ap, w_ap, rhs_main[k], start=(k == 0), stop=(k == K - 1))
                    if hs_ap is not None:
                        ins = t.matmul(hs_ap, w_ap, halo_rhs[k], start=(k == 0), stop=(k == K - 1))
                return ins

            writes = [b_bank[bk]] + ([b_bank[hs]] if hs is not None else [])
            P.op("pe", fn, reads=[b_slot[s]] + list(rhs_bufs), writes=writes)
            return bk, hs

        P.op("sp", lambda e: e.dma_start(out=prm[:], in_=prm_d), writes=[b_prm], dma_sem="prm")
        P.op("sp", lambda e: e.dma_start(out=pos[:], in_=pos_d), writes=[b_pos], dma_sem="pos")
        P.op("dve", lambda v: v.memset(ones[:], 1.0), writes=[b_ones])
        for hf in range(NHALF):
            for g, w in enumerate(WINDOWS):
                o = (hf * 4 + g) * 16
                P.op("dve", lambda v, o=o, hf=hf, w=w: v.tensor_scalar(
                    out=invc[:, o:o + 16], in0=pos[:, hf * 16:(hf + 1) * 16], scalar1=float(w), scalar2=None,
                    op0=ALU.min), reads=[b_pos], writes=[b_invc])
        P.op("dve", lambda v: v.reciprocal(out=invc[:], in_=invc[:]), reads=[b_invc], writes=[b_invc])

        def phase0(hf, resident):
            SQ, bSQ = [P0t[NXS], P0t[NXS + 1]], [b_P0[NXS], b_P0[NXS + 1]]
            ACC, bACC = P0t[NXS + 2], b_P0[NXS + 2]
            RSA, bRSA = P0t[NXS + 3], b_P0[NXS + 3]
            RSB, bRSB = P0t[NXS + 4], b_P0[NXS + 4]
            if resident:
                for cg in range(8):
                    lo, hi = cg * 4 * TT, (cg + 1) * 4 * TT
                    P.op("sp",
                         lambda e, lo=lo, hi=hi: e.dma_start(out=AM[:, lo:hi], in_=xT_d[hf][:, lo:hi]),
                         writes=b_xt0[cg * 4:(cg + 1) * 4], dma_sem=f"xl{cg}")
            PF = NXS - 1
            scnt = [0]

            def issue_load(c):
                i = scnt[0] % NXS
                scnt[0] += 1
                P.op("sp", lambda e, i=i, c=c: e.dma_start(out=P0t[i], in_=xT_d[hf][:, c * TT:(c + 1) * TT]),
                     writes=[b_P0[i]], dma_sem=f"xs{i}")
                return i

            pend = []

            def start_pass():
                if not resident:
                    for c in range(PF):
                        pend.append(issue_load(c))

            def get_x(c):
                if resident:
                    return XT0[c], b_xt0[c]
                if c + PF < NCH:
                    pend.append(issue_load(c + PF))
                i = pend.pop(0)
                return P0t[i], b_P0[i]

            start_pass()
            for c in range(NCH):
                xt, bxt = get_x(c)
                if c == 0:
                    P.op("act", lambda a, xt=xt: a.activation(out=ACC, in_=xt, func=AF.Square),
                         reads=[bxt], writes=[bACC])
                else:
                    q = c % 2
                    P.op("act", lambda a, xt=xt, q=q: a.activation(out=SQ[q], in_=xt, func=AF.Square),
                         reads=[bxt], writes=[bSQ[q]])
                    P.op("dve", lambda v, q=q: v.tensor_tensor(out=ACC, in0=ACC, in1=SQ[q], op=ALU.add),
                         reads=[bACC, bSQ[q]], writes=[bACC])
                yield
            bk0, hb0 = next_bank(), next_hs()

            def fn(t):
                t.matmul(ps[bk0][:, 0:T], ones[:], ACC[:, HALO:TT], start=True, stop=True)
                return t.matmul(ps[hb0][:, 0:HALO], ones[:], ACC[:, 0:HALO], start=True, stop=True)
            P.op("pe", fn, reads=[b_ones, bACC], writes=[b_bank[bk0], b_bank[hb0]])
            P.op("dve", lambda v: v.tensor_scalar(out=RSA[:, HALO:TT], in0=ps[bk0][:, 0:T], scalar1=1.0 / D,
                                                 scalar2=EPS, op0=ALU.mult, op1=ALU.add),
                 reads=[b_bank[bk0]], writes=[bRSA])
            P.op("dve", lambda v: v.tensor_scalar(out=RSA[:, 0:HALO], in0=ps[hb0][:, 0:HALO], scalar1=1.0 / D,
                                                 scalar2=EPS, op0=ALU.mult, op1=ALU.add),
                 reads=[b_bank[hb0]], writes=[bRSA])
            P.op("act", lambda a: a.activation(out=RSB, in_=RSA, func=AF.Sqrt), reads=[bRSA], writes=[bRSB])
            P.op("dve", lambda v: v.reciprocal(out=RSA, in_=RSB), reads=[bRSB], writes=[bRSA])
            start_pass()
            yield
            for c in range(NCH):
                xt, bxt = get_x(c)
                P.op("dve", lambda v, c=c, xt=xt: v.scalar_tensor_tensor(
                    out=hT(c), in0=xt, scalar=pcol(PC_NORMW + c), in1=RSA, op0=ALU.mult, op1=ALU.mult),
                    reads=[bxt, bRSA, b_prm], writes=[b_hT[c]])
                yield


        out_sigs = []
        for _ in phase0(0, True):
            pass
        for hf in range(NHALF):
            first = (hf == 0)
            h_main = [hT(c, HALO, TT) for c in range(NCH)]
            h_lo = [hT(c, 0, HW) for c in range(NCH)]
            h_hi = [hT(c, HW, TT) for c in range(NCH)]

            def win_unit(col_chunk, halo, split=False):
                s = load_unit(win_d[col_chunk], 4096)
                if halo:
                    return mm_unit(s, NCH, h_lo, b_hT, halo_rhs=h_hi, split=split, nmain=HW, nhalo=HW)
                return mm_unit(s, NCH, h_main, b_hT, split=split)

            def win_units_interleaved(col_chunks):
                n = len(col_chunks)
                slots = [load_unit(win_d[cc], 4096) for cc in col_chunks]
                banks = []
                for i in range(n):
                    if i < 2:
                        banks.append((next_bank(), next_hs()))
                    else:
                        banks.append((next_bank(), next_bank()))
                for k in range(NCH):
                    for i in range(n):
                        wt = wring[slots[i]]
                        bk, hs = banks[i]

                        def fnk(t, k=k, wt=wt, bk=bk, hs=hs):
                            w_ap = wt[:, k * 128:(k + 1) * 128]
                            t.matmul(ps[bk][:, 0:HW], w_ap, h_lo[k], start=(k == 0), stop=(k == NCH - 1))
                            return t.matmul(ps[hs][:, 0:HW], w_ap, h_hi[k], start=(k == 0), stop=(k == NCH - 1))
                        P.op("pe", fnk, reads=[b_slot[slots[i]], b_hT[k]], writes=[b_bank[bk], b_bank[hs]])
                return banks

            def conv_round(j, hf=hf, first=first):
                ui, vi, yi, zi = j % 2, 2 + j % 2, j % 2, 2 + j % 2
                U, V, Y, Z = W528[ui], W528[vi], W512[yi], W512[zi]
                bU, bV, bY, bZ = b_W528[ui], b_W528[vi], b_W512[yi], b_W512[zi]
                sv = saveV[:, j * 16:(j + 1) * 16]
                bk, hs = win_unit(2 * NPC + j, first)
                if first:
                    P.op("act", lambda a, bk=bk: a.activation(out=U[:, 0:HW], in_=ps[bk][:, 0:HW], func=AF.Copy),
                         reads=[b_bank[bk]], writes=[bU])
                    P.op("act", lambda a, hs=hs: a.activation(out=U[:, HW:TT], in_=ps[hs][:, 0:HW], func=AF.Copy),
                         reads=[b_bank[hs]], writes=[bU])
                else:
                    P.op("act", lambda a, bk=bk: a.activation(out=U[:, HALO:TT], in_=ps[bk][:, 0:T], func=AF.Copy),
                         reads=[b_bank[bk]], writes=[bU])
                bk, hs = win_unit(4 * NPC + j, first)
                if first:
                    P.op("dve", lambda v, bk=bk: v.tensor_tensor(out=V[:, 0:HW], in0=ps[bk][:, 0:HW], in1=U[:, 0:HW],
                                                                op=ALU.mult), reads=[b_bank[bk], bU], writes=[bV])
                    P.op("dve", lambda v, hs=hs: v.tensor_tensor(out=V[:, HW:TT], in0=ps[hs][:, 0:HW],
                                                                in1=U[:, HW:TT], op=ALU.mult),
                         reads=[b_bank[hs], bU], writes=[bV])
                    P.op("act", lambda a: a.activation(out=sv, in_=V[:, T:TT], func=AF.Copy),
                         reads=[bV], writes=[b_saveV[j]])
                else:
                    P.op("dve", lambda v, bk=bk: v.tensor_tensor(out=V[:, HALO:TT], in0=ps[bk][:, 0:T],
                                                                in1=U[:, HALO:TT], op=ALU.mult),
                         reads=[b_bank[bk], bU], writes=[bV])
                    P.op("dve", lambda v: v.tensor_copy(out=V[:, 0:HALO], in_=sv), reads=[b_saveV[j]], writes=[bV])
                P.op("act", lambda a: a.activation(out=Y, in_=V[:, HALO:TT], func=AF.Identity,
                                                   bias=pcol(PC_CB + j), scale=pcol(PC_CW2 + j)),
                     reads=[bV, b_prm], writes=[bY])
                P.op("dve", lambda v: v.scalar_tensor_tensor(out=Y, in0=V[:, HALO - 1:TT - 1], scalar=pcol(PC_CW1 + j),
                                                             in1=Y, op0=ALU.mult, op1=ALU.add),
                     reads=[bV, bY, b_prm], writes=[bY])
                P.op("dve", lambda v: v.scalar_tensor_tensor(out=Y, in0=V[:, HALO - 2:TT - 2], scalar=pcol(PC_CW0 + j),
                                                             in1=Y, op0=ALU.mult, op1=ALU.add),
                     reads=[bV, bY, b_prm], writes=[bY])
                bk, _ = win_unit(3 * NPC + j, False)
                P.op("dve", lambda v, bk=bk: v.tensor_tensor(out=Y, in0=ps[bk][:, 0:T], in1=Y, op=ALU.mult),
                     reads=[b_bank[bk], bY], writes=[bY])
                bk, _ = win_unit(5 * NPC + j, False)
                P.op("act", lambda a, bk=bk: a.activation(out=Z, in_=ps[bk][:, 0:T], func=AF.Silu),
                     reads=[b_bank[bk]], writes=[bZ])
                P.op("dve", lambda v: v.tensor_tensor(out=YS[NPC + j], in0=Y, in1=Z, op=ALU.mult),
                     reads=[bY, bZ], writes=[b_ys[NPC + j]])

            for g in range(4):
                w = WINDOWS[g]
                L = g + 1
                pre = win_units_interleaved([0, 1, 2, 3]) if (first and g == 0) else None
                for jj in range(4):
                    j = 4 * g + jj
                    ui = 4 + j % 2
                    U, bU = W528[ui], b_W528[ui]
                    su = saveU[:, j * 16:(j + 1) * 16]
                    if pre is not None:
                        bk, hs = pre[jj]
                    else:
                        bk, hs = win_unit(j, first)
                    if first:
                        P.op("act", lambda a, bk=bk, U=U: a.activation(out=U[:, 0:HW], in_=ps[bk][:, 0:HW],
                                                                      func=AF.Copy),
                             reads=[b_bank[bk]], writes=[bU])
                        P.op("act", lambda a, hs=hs, U=U: a.activation(out=U[:, HW:TT], in_=ps[hs][:, 0:HW],
                                                                      func=AF.Copy),
                             reads=[b_bank[hs]], writes=[bU])
                        P.op("act", lambda a, U=U, su=su: a.activation(out=su, in_=U[:, T:TT], func=AF.Copy),
                             reads=[bU], writes=[b_saveU[j]])
                    else:
                        P.op("act", lambda a, bk=bk, U=U: a.activation(out=U[:, HALO:TT], in_=ps[bk][:, 0:T],
                                                                      func=AF.Copy),
                             reads=[b_bank[bk]], writes=[bU])
                        P.op("act", lambda a, U=U, su=su: a.activation(out=U[:, 0:HALO], in_=su, func=AF.Copy),
                             reads=[b_saveU[j]], writes=[bU])
                    src, bsrc = U, bU
                    for l in range(1, L + 1):
                        lo = 2 ** l - 1
                        sh = 2 ** (l - 1)
                        di = l % 2
                        dst, bdst = P0t[di], b_P0[di]
                        P.op("dve", lambda v, dst=dst, src=src, lo=lo, sh=sh: v.tensor_tensor(
                            out=dst[:, lo:TT], in0=src[:, lo:TT], in1=src[:, lo - sh:TT - sh], op=ALU.add),
                            reads=[bsrc], writes=[bdst])
                        src, bsrc = dst, bdst
                    pl = pooled[:, jj * T:(jj + 1) * T]
                    P.op("dve", lambda v, pl=pl, src=src, U=U, w=w: v.scalar_tensor_tensor(
                        out=pl, in0=src[:, HALO:TT], scalar=1.0 / w, in1=U[:, HALO:TT], op0=ALU.mult,
                        op1=ALU.subtract), reads=[bsrc, bU], writes=[b_pooled[jj]])
                    io = (hf * 4 + g) * 16
                    P.op("dve", lambda v, src=src, io=io: v.tensor_tensor(
                        out=tmp16[:], in0=src[:, HALO:HALO + 16], in1=invc[:, io:io + 16], op=ALU.mult),
                        reads=[bsrc, b_invc], writes=[b_tmp16])
                    P.op("dve", lambda v, pl=pl, U=U: v.tensor_tensor(
                        out=pl[:, 0:16], in0=tmp16[:], in1=U[:, HALO:HALO + 16], op=ALU.subtract),
                        reads=[b_tmp16, bU], writes=[b_pooled[jj]])
                for jj in range(4):
                    j = 4 * g + jj
                    zi = 4 + jj
                    bk, _ = win_unit(NPC + j, False)
                    P.op("act", lambda a, bk=bk, zi=zi: a.activation(out=W512[zi], in_=ps[bk][:, 0:T], func=AF.Silu),
                         reads=[b_bank[bk]], writes=[b_W512[zi]])
                conv_round(4 * g)
                s = load_unit(pw_d[g], 2048)
                p_rhs = [pooled[:, kc * T:(kc + 1) * T] for kc in range(4)]
                for i in range(4):
                    j = 4 * g + i
                    bk, _ = mm_unit(s, 4, p_rhs, b_pooled, woff=i * 512)
                    P.op("dve", lambda v, bk=bk, j=j, i=i: v.scalar_tensor_tensor(
                        out=YS[j], in0=ps[bk][:, 0:T], scalar=pcol(PC_PSCALE + j), in1=W512[4 + i],
                        op0=ALU.mult, op1=ALU.mult),
                        reads=[b_bank[bk], b_W512[4 + i], b_prm], writes=[b_ys[j]])
                conv_round(4 * g + 1)
                conv_round(4 * g + 2)
                conv_round(4 * g + 3)

            ys_lo = [YS[k] for k in range(NPC)]
            ys_hi = [YS[NPC + k] for k in range(NPC)]
            nxt = phase0(hf + 1, False) if hf + 1 < NHALF else None

            def preload_xn(groups, hf=hf):
                for eg in groups:
                    e0 = eg * 4
                    src = xTm_d[hf][:, e0 * T:(e0 + 4) * T]
                    dst = AM[:, e0 * T:(e0 + 4) * T]
                    P.op("sp", lambda q, src=src, dst=dst: q.dma_start(out=dst, in_=src),
                         writes=b_xn[e0:e0 + 4], dma_sem=f"xr{eg}")

            preload_xn(range(4, 8))

            def advance(n=1, nxt=nxt):
                if nxt is None:
                    return
                for _ in range(n):
                    try:
                        next(nxt)
                    except StopIteration:
                        return

            for m in range(NCH):
                g0i, g1i = (m % 2), 2 + (m % 2)
                G0, G1 = S3[g0i], S3[g1i]
                bG0, bG1 = b_S3[g0i], b_S3[g1i]
                bk, _ = win_unit(6 * NPC + m, False)
                P.op("act", lambda a, bk=bk, G0=G0, m=m: a.activation(out=G0, in_=ps[bk][:, 0:T], func=AF.Sigmoid,
                                                                     bias=pcol(PC_GB0 + m)),
                     reads=[b_bank[bk], b_prm], writes=[bG0])
                bk, _ = win_unit(6 * NPC + NCH + m, False)
                P.op("act", lambda a, bk=bk, G1=G1, m=m: a.activation(out=G1, in_=ps[bk][:, 0:T], func=AF.Sigmoid,
                                                                     bias=pcol(PC_GB1 + m)),
                     reads=[b_bank[bk], b_prm], writes=[bG1])
                s = load_unit(wbr_d[0, m], 2048)
                bk, _ = mm_unit(s, NPC, ys_lo, b_ys[:NPC])
                P.op("dve", lambda v, bk=bk, G0=G0: v.tensor_tensor(out=G0, in0=ps[bk][:, 0:T], in1=G0, op=ALU.mult),
                     reads=[b_bank[bk], bG0], writes=[bG0])
                s = load_unit(wbr_d[1, m], 2048)
                bk, _ = mm_unit(s, NPC, ys_hi, b_ys[NPC:])
                P.op("dve", lambda v, bk=bk, G1=G1: v.tensor_tensor(out=G1, in0=ps[bk][:, 0:T], in1=G1, op=ALU.mult),
                     reads=[b_bank[bk], bG1], writes=[bG1])
                P.op("dve", lambda v, G0=G0, G1=G1, m=m: v.tensor_tensor(out=MG[m], in0=G0, in1=G1, op=ALU.add),
                     reads=[bG0, bG1], writes=[b_mg[m]])
                advance(1)
            advance(1)

            preload_xn(range(0, 4))
            mg_rhs = [MG[k] for k in range(NCH)]
            SQ3, bSQ3 = [S3[0], S3[1]], [b_S3[0], b_S3[1]]
            ACC3, bACC3 = S3[2], b_S3[2]
            RS3, bRS3 = S3[3], b_S3[3]

            for ei, e in enumerate(list(range(16, 32)) + list(range(0, 16))):
                s = load_unit(wo_d[e], 4096)
                bk, _ = mm_unit(s, NCH, mg_rhs, b_mg)
                P.op("dve", lambda v, bk=bk, e=e: v.tensor_tensor(out=XN[e], in0=ps[bk][:, 0:T], in1=XN[e],
                                                                  op=ALU.add),
                     reads=[b_bank[bk], b_xn[e]], writes=[b_xn[e]])
                if ei == 0:
                    P.op("act", lambda a, e=e: a.activation(out=ACC3, in_=XN[e], func=AF.Square),
                         reads=[b_xn[e]], writes=[bACC3])
                else:
                    q = e % 2
                    P.op("act", lambda a, e=e, q=q: a.activation(out=SQ3[q], in_=XN[e], func=AF.Square),
                         reads=[b_xn[e]], writes=[bSQ3[q]])
                    P.op("dve", lambda v, q=q: v.tensor_tensor(out=ACC3, in0=ACC3, in1=SQ3[q], op=ALU.add),
                         reads=[bACC3, bSQ3[q]], writes=[bACC3])
                advance(1)
            advance(100)

            def fn(t):
                return t.matmul(ps[7][:, 0:T], ones[:], ACC3, start=True, stop=True)
            P.op("pe", fn, reads=[b_ones, bACC3], writes=[b_bank[7]])
            P.op("dve", lambda v: v.tensor_scalar(out=RS3, in0=ps[7][:, 0:T], scalar1=1.0 / D, scalar2=EPS,
                                                 op0=ALU.mult, op1=ALU.add), reads=[b_bank[7]], writes=[bRS3])
            P.op("act", lambda a: a.activation(out=ACC3, in_=RS3, func=AF.Sqrt), reads=[bRS3], writes=[bACC3])
            P.op("dve", lambda v: v.reciprocal(out=RS3, in_=ACC3), reads=[bACC3], writes=[bRS3])
            last = (hf == NHALF - 1)
            for og in (4, 5, 6, 7, 0, 1, 2, 3):
                for e in range(og * 4, (og + 1) * 4):
                    if False and last and (e % 8) in (1, 4, 6):
                        P.op("act", lambda a, e=e: a.activation(out=XN[e], in_=XN[e], func=AF.Copy,
                                                                scale=pcol(PC_FNW + e)),
                             reads=[b_xn[e], b_prm], writes=[b_xn[e]])
                        P.op("pool", lambda g, e=e: g.tensor_tensor(out=XN[e], in0=XN[e], in1=RS3, op=ALU.mult),
                             reads=[b_xn[e], bRS3], writes=[b_xn[e]])
                    else:
                        P.op("dve", lambda v, e=e: v.scalar_tensor_tensor(out=XN[e], in0=XN[e],
                                                                          scalar=pcol(PC_FNW + e), in1=RS3,
                                                                          op0=ALU.mult, op1=ALU.mult),
                             reads=[b_xn[e], bRS3, b_prm], writes=[b_xn[e]])
                lo, hi = og * 4 * T, (og + 1) * 4 * T
                sig = P.op("sp", lambda q, lo=lo, hi=hi, hf=hf: q.dma_start(out=out_d[hf][:, lo:hi],
                                                                            in_=AM[:, lo:hi]),
                           reads=b_xn[og * 4:(og + 1) * 4], dma_sem=f"o{og}")
                out_sigs.append(sig)

        final_waits = {}
        for key, val in out_sigs:
            final_waits[key] = max(final_waits.get(key, 0), val)
        P.emit(block, sems, list(final_waits.items()))
    return nc


def _prep_inputs(x, norm_w, w_in, pool_w, pool_scale, conv_w, conv_b, gate_b, w_branch, w_out, final_norm_w):
    f = np.float32
    x = np.asarray(x, f)
    B, S, _ = x.shape
    xf = x.reshape(B * S, D)
    per_core = (B * S) // NCORES

    def colmat(v, n):
        return np.asarray(v, f).reshape(n, 128).T

    prm = np.zeros((128, NPRM), f)
    prm[:, PC_NORMW:PC_NORMW + 32] = colmat(norm_w[0], 32)
    prm[:, PC_PSCALE:PC_PSCALE + 16] = colmat(pool_scale[0], 16)
    prm[:, PC_CW0:PC_CW0 + 16] = colmat(conv_w[0, 0], 16)
    prm[:, PC_CW1:PC_CW1 + 16] = colmat(conv_w[0, 1], 16)
    prm[:, PC_CW2:PC_CW2 + 16] = colmat(conv_w[0, 2], 16)
    prm[:, PC_CB:PC_CB + 16] = colmat(conv_b[0], 16)
    prm[:, PC_GB0:PC_GB0 + 32] = colmat(gate_b[0, 0], 32)
    prm[:, PC_GB1:PC_GB1 + 32] = colmat(gate_b[0, 1], 32)
    prm[:, PC_FNW:PC_FNW + 32] = colmat(final_norm_w, 32)

    win = np.ascontiguousarray(
        np.asarray(w_in[0], f).reshape(32, 128, 160, 128).transpose(2, 1, 0, 3)).reshape(160, 128, 4096)
    pw = np.ascontiguousarray(
        np.asarray(pool_w[0], f).reshape(4, 4, 128, 4, 128).transpose(0, 2, 3, 1, 4)).reshape(4, 128, 2048)
    wbr = np.ascontiguousarray(
        np.asarray(w_branch[0], f).reshape(2, 16, 128, 32, 128).transpose(0, 3, 2, 1, 4)).reshape(2, 32, 128, 2048)
    wo = np.ascontiguousarray(
        np.asarray(w_out[0], f).reshape(32, 128, 32, 128).transpose(2, 1, 0, 3)).reshape(32, 128, 4096)

    in_maps = []
    for c in range(NCORES):
        xT = np.zeros((NHALF, 128, NCH * TT), f)
        xTm = np.zeros((NHALF, 128, NCH * T), f)
        pos = np.zeros((128, NHALF * 16), f)
        for hf in range(NHALF):
            t0 = c * per_core + hf * T
            s0 = t0 % S
            blk = np.zeros((TT, D), f)
            blk[HALO:] = xf[t0:t0 + T]
            if s0 > 0:
                blk[:HALO] = xf[t0 - HALO:t0]
            xT[hf] = blk.T.reshape(NCH, 128, TT).transpose(1, 0, 2).reshape(128, NCH * TT)
            xTm[hf] = blk[HALO:].T.reshape(NCH, 128, T).transpose(1, 0, 2).reshape(128, NCH * T)
            pos[:, hf * 16:(hf + 1) * 16] = (s0 + 1 + np.arange(16, dtype=f))[None, :]
        in_maps.append({"xT": xT, "xTm": xTm, "prm": prm, "pos": pos, "win": win, "pw": pw, "wbr": wbr, "wo": wo})
    return in_maps, (B, S)


_NC_CACHE = {}


def kernel(x, norm_w, w_in, pool_w, pool_scale, conv_w, conv_b, gate_b, w_branch, w_out, final_norm_w):
    in_maps, (B, S) = _prep_inputs(x, norm_w, w_in, pool_w, pool_scale, conv_w, conv_b, gate_b, w_branch, w_out,
                                   final_norm_w)
    if "nc" not in _NC_CACHE:
        _NC_CACHE["nc"] = build_program()
    nc = _NC_CACHE["nc"]
    res = run_bass_kernel_spmd(nc, in_maps, core_ids=list(range(NCORES)))
    outs = []
    for c in range(NCORES):
        o = np.asarray(res.results[c]["outT"]).reshape(NHALF, 128, NCH, T)
        outs.append(o.transpose(0, 3, 2, 1).reshape(NHALF * T, D))
    return np.concatenate(outs, axis=0).reshape(B, S, D).astype(np.float32)
```

```python
import contextlib
import numpy as np
import concourse.bass as bass
import concourse.mybir as mybir
from concourse.bass_utils import run_bass_kernel_spmd

F32 = mybir.dt.float32
BF16 = mybir.dt.bfloat16
AF = mybir.ActivationFunctionType
ALU = mybir.AluOpType

NCORES = 8
D = 4096
NCH = D // 128
PWD = 2048
NPC = PWD // 128
T = 512
HALO = 16
TT = T + HALO
HW = TT // 2
NHALF = 2
NSLOT = 5
NXS = 5
NP0 = NXS + 5
EPS = 1e-6
WINDOWS = (2, 4, 8, 16)

PC_NORMW = 0
PC_PSCALE = 32
PC_CW0 = 48
PC_CW1 = 64
PC_CW2 = 80
PC_CB = 96
PC_GB0 = 112
PC_GB1 = 144
PC_FNW = 176
NPRM = 208

COMPUTE = ("pe", "act", "dve", "pool")
ENGS = ("pe", "act", "dve", "sp", "pool")


class Region:
    def __init__(self, name):
        self.name = name
        self.bufs = []


class Buf:
    def __init__(self, name, region=None, lo=0, hi=0):
        self.name = name
        self.w = None
        self.r = []
        self.lo, self.hi = lo, hi
        self.aliases = []
        if region is not None:
            for o in region.bufs:
                if o.lo < hi and lo < o.hi:
                    o.aliases.append(self)
                    self.aliases.append(o)
            region.bufs.append(self)


class Prog:
    def __init__(self):
        self.ops = {e: [] for e in ENGS}
        self.seq = {e: 0 for e in COMPUTE}
        self.waited = {e: {} for e in ENGS}
        self.needed = {e: set() for e in COMPUTE}
        self.dma_cnt = {}

    def op(self, eng, fn, reads=(), writes=(), dma_sem=None):
        deps = {}

        def add(sig, same_ok):
            if sig is None:
                return
            key, val = sig
            if key == eng and not same_ok:
                return
            if deps.get(key, 0) < val:
                deps[key] = val

        wset = []
        for b in writes:
            wset.append(b)
            wset.extend(b.aliases)
        for b in reads:
            add(b.w, True)
            for o in b.aliases:
                add(o.w, True)
        for b in wset:
            add(b.w, eng != "pe")
            for s in b.r:
                add(s, False)
        waits = []
        wd = self.waited[eng]
        for key, val in deps.items():
            if wd.get(key, 0) >= val:
                continue
            wd[key] = val
            waits.append((key, val))
            if key in COMPUTE:
                self.needed[key].add(val)
        if dma_sem is None:
            self.seq[eng] += 1
            sig = (eng, self.seq[eng])
        else:
            self.dma_cnt[dma_sem] = self.dma_cnt.get(dma_sem, 0) + 16
            sig = (dma_sem, self.dma_cnt[dma_sem])
        self.ops[eng].append((waits, fn, sig))
        for b in reads:
            b.r.append(sig)
        for b in wset:
            b.w = sig
            b.r = []
        return sig

    def emit(self, block, sems, final_waits):
        ranks = {}
        for e in COMPUTE:
            ranks[e] = {v: i + 1 for i, v in enumerate(sorted(self.needed[e]))}

        def tr(key, val):
            if key in COMPUTE:
                return ranks[key][val]
            return val

        def make_body(eng):
            ops = self.ops[eng]

            def body(e):
                for waits, fn, sig in ops:
                    for key, val in waits:
                        e.wait_ge(sems[key], tr(key, val))
                    ins = fn(e)
                    if sig[0] in COMPUTE:
                        if sig[1] in ranks[eng]:
                            ins.then_inc(sems[eng], 1)
                    else:
                        ins.then_inc(sems[sig[0]], 16)
                if eng == "sp":
                    for key, val in final_waits:
                        e.wait_ge(sems[key], tr(key, val))
            return body

        block.tensor(make_body("pe"))
        block.scalar(make_body("act"))
        block.vector(make_body("dve"))
        block.sync(make_body("sp"))
        block.gpsimd(make_body("pool"))


def build_program():
    nc = bass.Bass("TRN2", target_bir_lowering=False)
    xT_d = nc.dram_tensor("xT", [NHALF, 128, NCH * TT], F32, kind="ExternalInput").ap()
    xTm_d = nc.dram_tensor("xTm", [NHALF, 128, NCH * T], F32, kind="ExternalInput").ap()
    prm_d = nc.dram_tensor("prm", [128, NPRM], F32, kind="ExternalInput").ap()
    pos_d = nc.dram_tensor("pos", [128, NHALF * 16], F32, kind="ExternalInput").ap()
    win_d = nc.dram_tensor("win", [160, 128, 4096], F32, kind="ExternalInput").ap()
    pw_d = nc.dram_tensor("pw", [4, 128, 2048], F32, kind="ExternalInput").ap()
    wbr_d = nc.dram_tensor("wbr", [2, 32, 128, 2048], F32, kind="ExternalInput").ap()
    wo_d = nc.dram_tensor("wo", [32, 128, 4096], F32, kind="ExternalInput").ap()
    out_d = nc.dram_tensor("outT", [NHALF, 128, NCH * T], F32, kind="ExternalOutput").ap()

    P = Prog()
    es = contextlib.ExitStack()
    with es:
        AM = es.enter_context(nc.sbuf_tensor("AM", [128, 24576], F32))
        Hh = es.enter_context(nc.sbuf_tensor("Hh", [128, 8448], F32))
        P0 = es.enter_context(nc.sbuf_tensor("P0", [128, NP0 * TT], F32))
        S3t = es.enter_context(nc.sbuf_tensor("S3t", [128, 4 * T], F32))
        wring = [es.enter_context(nc.sbuf_tensor(f"wr{i}", [128, 4096], BF16)) for i in range(NSLOT)]
        pooled = es.enter_context(nc.sbuf_tensor("pooled", [128, 4 * T], BF16))
        prm = es.enter_context(nc.sbuf_tensor("prm_sb", [128, NPRM], F32))
        pos = es.enter_context(nc.sbuf_tensor("pos_sb", [128, NHALF * 16], F32))
        invc = es.enter_context(nc.sbuf_tensor("invc", [128, NHALF * 4 * 16], F32))
        ones = es.enter_context(nc.sbuf_tensor("ones", [128, 128], F32))
        tmp16 = es.enter_context(nc.sbuf_tensor("tmp16", [128, 16], F32))
        saveU = es.enter_context(nc.sbuf_tensor("saveU", [128, NPC * 16], F32))
        saveV = es.enter_context(nc.sbuf_tensor("saveV", [128, NPC * 16], F32))
        ps = [es.enter_context(nc.psum_tensor(f"ps{i}", [128, 512], F32)) for i in range(8)]

        sem_names = list(COMPUTE) + [f"w{i}" for i in range(NSLOT)] + [f"xl{i}" for i in range(8)] + \
            [f"xs{i}" for i in range(NXS)] + [f"xr{i}" for i in range(8)] + [f"o{i}" for i in range(8)] + \
            ["prm", "pos"]
        sems = {n: es.enter_context(nc.semaphore(n)) for n in sem_names}
        block = es.enter_context(nc.Block())

        rAM, rH, rP0, rS3 = Region("AM"), Region("H"), Region("P0"), Region("S3")

        def f32view(t, blo, n):
            return t[:, blo // 4:blo // 4 + n]

        def bf16view(t, blo, n):
            return t[:, blo // 4:(blo + 2 * n) // 4].bitcast(BF16)

        XN = [f32view(AM, e * 2048, T) for e in range(NCH)]
        b_xn = [Buf(f"xn{e}", rAM, e * 2048, (e + 1) * 2048) for e in range(NCH)]
        YS = [bf16view(AM, j * 1024, T) for j in range(NCH)]
        b_ys = [Buf(f"ys{j}", rAM, j * 1024, (j + 1) * 1024) for j in range(NCH)]
        SCR0 = 32768
        W528 = [f32view(AM, SCR0 + i * 2112, TT) for i in range(6)]
        b_W528 = [Buf(f"W528_{i}", rAM, SCR0 + i * 2112, SCR0 + (i + 1) * 2112) for i in range(6)]
        W512_0 = SCR0 + 6 * 2112
        W512 = [f32view(AM, W512_0 + i * 2048, T) for i in range(8)]
        b_W512 = [Buf(f"W512_{i}", rAM, W512_0 + i * 2048, W512_0 + (i + 1) * 2048) for i in range(8)]
        assert W512_0 + 8 * 2048 <= 65536
        MG0 = 65536
        MG = [bf16view(AM, MG0 + m * 1024, T) for m in range(NCH)]
        b_mg = [Buf(f"mg{m}", rAM, MG0 + m * 1024, MG0 + (m + 1) * 1024) for m in range(NCH)]
        XT0 = [f32view(AM, c * 2112, TT) for c in range(NCH)]
        b_xt0 = [Buf(f"xt0_{c}", rAM, c * 2112, (c + 1) * 2112) for c in range(NCH)]

        hT_all = Hh[:, :].bitcast(BF16)

        def hT(c, lo=0, hi=TT):
            return hT_all[:, c * TT + lo:c * TT + hi]
        b_hT = [Buf(f"hT{c}", rH, c * 1056, (c + 1) * 1056) for c in range(NCH)]

        P0t = [f32view(P0, i * 2112, TT) for i in range(NP0)]
        b_P0 = [Buf(f"P0_{i}", rP0, i * 2112, (i + 1) * 2112) for i in range(NP0)]
        S3 = [f32view(S3t, i * 2048, T) for i in range(4)]
        b_S3 = [Buf(f"S3_{i}", rS3, i * 2048, (i + 1) * 2048) for i in range(4)]

        b_slot = [Buf(f"slot{i}") for i in range(NSLOT)]
        b_pooled = [Buf(f"pooled{i}") for i in range(4)]
        b_prm, b_pos, b_invc, b_ones, b_tmp16 = Buf("prm"), Buf("pos"), Buf("invc"), Buf("ones"), Buf("tmp16")
        b_saveU = [Buf(f"saveU{j}") for j in range(NPC)]
        b_saveV = [Buf(f"saveV{j}") for j in range(NPC)]
        b_bank = [Buf(f"bank{i}") for i in range(8)]

        def pcol(c):
            return prm[:, c:c + 1]

        state = {"bank": 0, "hs": 0, "unit": 0}

        def next_bank():
            i = state["bank"]
            state["bank"] = (i + 1) % 6
            return i

        def next_hs():
            i = state["hs"]
            state["hs"] = (i + 1) % 2
            return 6 + i

        def load_unit(src_ap, ncols):
            s = state["unit"] % NSLOT
            state["unit"] += 1
            dst = wring[s][:, 0:ncols]
            rd = list(b_xt0[:20]) if state["unit"] == 1 else []
            P.op("pool", lambda g, dst=dst, src_ap=src_ap: g.dma_start(out=dst, in_=src_ap, max_dma_last_dim=8192),
                 reads=rd, writes=[b_slot[s]], dma_sem=f"w{s}")
            return s

        def mm_unit(s, K, rhs_main, rhs_bufs, halo_rhs=None, woff=0, split=False, nmain=T, nhalo=HALO):
            bk = next_bank()
            hs = next_hs() if halo_rhs is not None else None
            wt = wring[s]
            bank_ap = ps[bk][:, 0:nmain]
            hs_ap = ps[hs][:, 0:nhalo] if hs is not None else None
            if split:
                for k in range(K):
                    def fnk(t, k=k):
                        w_ap = wt[:, woff + k * 128:woff + (k + 1) * 128]
                        ins = t.matmul(bank_ap, w_ap, rhs_main[k], start=(k == 0), stop=(k == K - 1))
                        if hs_ap is not None:
                            ins = t.matmul(hs_ap, w_ap, halo_rhs[k], start=(k == 0), stop=(k == K - 1))
                        return ins
                    P.op("pe", fnk, reads=[b_slot[s], rhs_bufs[k]],
                         writes=[b_bank[bk]] + ([b_bank[hs]] if hs is not None else []))
                return bk, hs

            def fn(t):
                ins = None
                for k in range(K):
                    w_ap = wt[:, woff + k * 128:woff + (k + 1) * 128]
                    ins = t.matmul(bank_ap, w_ap, rhs_main[k], start=(k == 0), stop=(k == K - 1))
                    if hs_ap is not None:
                        ins = t.matmul(hs_ap, w_ap, halo_rhs[k], start=(k == 0), stop=(k == K - 1))
                return ins

            writes = [b_bank[bk]] + ([b_bank[hs]] if hs is not None else [])
            P.op("pe", fn, reads=[b_slot[s]] + list(rhs_bufs), writes=writes)
            return bk, hs

        P.op("sp", lambda e: e.dma_start(out=prm[:], in_=prm_d), writes=[b_prm], dma_sem="prm")
        P.op("sp", lambda e: e.dma_start(out=pos[:], in_=pos_d), writes=[b_pos], dma_sem="pos")
        P.op("dve", lambda v: v.memset(ones[:], 1.0), writes=[b_ones])
        for hf in range(NHALF):
            for g, w in enumerate(WINDOWS):
                o = (hf * 4 + g) * 16
                P.op("dve", lambda v, o=o, hf=hf, w=w: v.tensor_scalar(
                    out=invc[:, o:o + 16], in0=pos[:, hf * 16:(hf + 1) * 16], scalar1=float(w), scalar2=None,
                    op0=ALU.min), reads=[b_pos], writes=[b_invc])
        P.op("dve", lambda v: v.reciprocal(out=invc[:], in_=invc[:]), reads=[b_invc], writes=[b_invc])

        def phase0(hf, resident):
            SQ, bSQ = [P0t[NXS], P0t[NXS + 1]], [b_P0[NXS], b_P0[NXS + 1]]
            ACC, bACC = P0t[NXS + 2], b_P0[NXS + 2]
            RSA, bRSA = P0t[NXS + 3], b_P0[NXS + 3]
            RSB, bRSB = P0t[NXS + 4], b_P0[NXS + 4]
            if resident:
                for cg in range(8):
                    lo, hi = cg * 4 * TT, (cg + 1) * 4 * TT
                    P.op("sp",
                         lambda e, lo=lo, hi=hi: e.dma_start(out=AM[:, lo:hi], in_=xT_d[hf][:, lo:hi]),
                         writes=b_xt0[cg * 4:(cg + 1) * 4], dma_sem=f"xl{cg}")
            PF = NXS - 1
            scnt = [0]

            def issue_load(c):
                i = scnt[0] % NXS
                scnt[0] += 1
                P.op("sp", lambda e, i=i, c=c: e.dma_start(out=P0t[i], in_=xT_d[hf][:, c * TT:(c + 1) * TT]),
                     writes=[b_P0[i]], dma_sem=f"xs{i}")
                return i

            pend = []

            def start_pass():
                if not resident:
                    for c in range(PF):
                        pend.append(issue_load(c))

            def get_x(c):
                if resident:
                    return XT0[c], b_xt0[c]
                if c + PF < NCH:
                    pend.append(issue_load(c + PF))
                i = pend.pop(0)
                return P0t[i], b_P0[i]

            start_pass()
            for c in range(NCH):
                xt, bxt = get_x(c)
                if c == 0:
                    P.op("act", lambda a, xt=xt: a.activation(out=ACC, in_=xt, func=AF.Square),
                         reads=[bxt], writes=[bACC])
                else:
                    q = c % 2
                    P.op("act", lambda a, xt=xt, q=q: a.activation(out=SQ[q], in_=xt, func=AF.Square),
                         reads=[bxt], writes=[bSQ[q]])
                    P.op("dve", lambda v, q=q: v.tensor_tensor(out=ACC, in0=ACC, in1=SQ[q], op=ALU.add),
                         reads=[bACC, bSQ[q]], writes=[bACC])
                yield
            bk0, hb0 = next_bank(), next_hs()

            def fn(t):
                t.matmul(ps[bk0][:, 0:T], ones[:], ACC[:, HALO:TT], start=True, stop=True)
                return t.matmul(ps[hb0][:, 0:HALO], ones[:], ACC[:, 0:HALO], start=True, stop=True)
            P.op("pe", fn, reads=[b_ones, bACC], writes=[b_bank[bk0], b_bank[hb0]])
            P.op("dve", lambda v: v.tensor_scalar(out=RSA[:, HALO:TT], in0=ps[bk0][:, 0:T], scalar1=1.0 / D,
                                                 scalar2=EPS, op0=ALU.mult, op1=ALU.add),
                 reads=[b_bank[bk0]], writes=[bRSA])
            P.op("dve", lambda v: v.tensor_scalar(out=RSA[:, 0:HALO], in0=ps[hb0][:, 0:HALO], scalar1=1.0 / D,
                                                 scalar2=EPS, op0=ALU.mult, op1=ALU.add),
                 reads=[b_bank[hb0]], writes=[bRSA])
            P.op("act", lambda a: a.activation(out=RSB, in_=RSA, func=AF.Sqrt), reads=[bRSA], writes=[bRSB])
            P.op("dve", lambda v: v.reciprocal(out=RSA, in_=RSB), reads=[bRSB], writes=[bRSA])
            start_pass()
            yield
            for c in range(NCH):
                xt, bxt = get_x(c)
                P.op("dve", lambda v, c=c, xt=xt: v.scalar_tensor_tensor(
                    out=hT(c), in0=xt, scalar=pcol(PC_NORMW + c), in1=RSA, op0=ALU.mult, op1=ALU.mult),
                    reads=[bxt, bRSA, b_prm], writes=[b_hT[c]])
                yield


        out_sigs = []
        for _ in phase0(0, True):
            pass
        for hf in range(NHALF):
            first = (hf == 0)
            h_main = [hT(c, HALO, TT) for c in range(NCH)]
            h_lo = [hT(c, 0, HW) for c in range(NCH)]
            h_hi = [hT(c, HW, TT) for c in range(NCH)]

            def win_unit(col_chunk, halo, split=False):
                s = load_unit(win_d[col_chunk], 4096)
                if halo:
                    return mm_unit(s, NCH, h_lo, b_hT, halo_rhs=h_hi, split=split, nmain=HW, nhalo=HW)
                return mm_unit(s, NCH, h_main, b_hT, split=split)

            def win_units_interleaved(col_chunks):
                n = len(col_chunks)
                slots = [load_unit(win_d[cc], 4096) for cc in col_chunks]
                banks = []
                for i in range(n):
                    if i < 2:
                        banks.append((next_bank(), next_hs()))
                    else:
                        banks.append((next_bank(), next_bank()))
                for k in range(NCH):
                    for i in range(n):
                        wt = wring[slots[i]]
                        bk, hs = banks[i]

                        def fnk(t, k=k, wt=wt, bk=bk, hs=hs):
                            w_ap = wt[:, k * 128:(k + 1) * 128]
                            t.matmul(ps[bk][:, 0:HW], w_ap, h_lo[k], start=(k == 0), stop=(k == NCH - 1))
                            return t.matmul(ps[hs][:, 0:HW], w_ap, h_hi[k], start=(k == 0), stop=(k == NCH - 1))
                        P.op("pe", fnk, reads=[b_slot[slots[i]], b_hT[k]], writes=[b_bank[bk], b_bank[hs]])
                return banks

            def conv_round(j, hf=hf, first=first):
                ui, vi, yi, zi = j % 2, 2 + j % 2, j % 2, 2 + j % 2
                U, V, Y, Z = W528[ui], W528[vi], W512[yi], W512[zi]
                bU, bV, bY, bZ = b_W528[ui], b_W528[vi], b_W512[yi], b_W512[zi]
                sv = saveV[:, j * 16:(j + 1) * 16]
                bk, hs = win_unit(2 * NPC + j, first)
                if first:
                    P.op("act", lambda a, bk=bk: a.activation(out=U[:, 0:HW], in_=ps[bk][:, 0:HW], func=AF.Copy),
                         reads=[b_bank[bk]], writes=[bU])
                    P.op("act", lambda a, hs=hs: a.activation(out=U[:, HW:TT], in_=ps[hs][:, 0:HW], func=AF.Copy),
                         reads=[b_bank[hs]], writes=[bU])
                else:
                    P.op("act", lambda a, bk=bk: a.activation(out=U[:, HALO:TT], in_=ps[bk][:, 0:T], func=AF.Copy),
                         reads=[b_bank[bk]], writes=[bU])
                bk, hs = win_unit(4 * NPC + j, first)
                if first:
                    P.op("dve", lambda v, bk=bk: v.tensor_tensor(out=V[:, 0:HW], in0=ps[bk][:, 0:HW], in1=U[:, 0:HW],
                                                                op=ALU.mult), reads=[b_bank[bk], bU], writes=[bV])
                    P.op("dve", lambda v, hs=hs: v.tensor_tensor(out=V[:, HW:TT], in0=ps[hs][:, 0:HW],
                                                                in1=U[:, HW:TT], op=ALU.mult),
                         reads=[b_bank[hs], bU], writes=[bV])
                    P.op("act", lambda a: a.activation(out=sv, in_=V[:, T:TT], func=AF.Copy),
                         reads=[bV], writes=[b_saveV[j]])
                else:
                    P.op("dve", lambda v, bk=bk: v.tensor_tensor(out=V[:, HALO:TT], in0=ps[bk][:, 0:T],
                                                                in1=U[:, HALO:TT], op=ALU.mult),
                         reads=[b_bank[bk], bU], writes=[bV])
                    P.op("dve", lambda v: v.tensor_copy(out=V[:, 0:HALO], in_=sv), reads=[b_saveV[j]], writes=[bV])
                P.op("act", lambda a: a.activation(out=Y, in_=V[:, HALO:TT], func=AF.Identity,
                                                   bias=pcol(PC_CB + j), scale=pcol(PC_CW2 + j)),
                     reads=[bV, b_prm], writes=[bY])
                P.op("dve", lambda v: v.scalar_tensor_tensor(out=Y, in0=V[:, HALO - 1:TT - 1], scalar=pcol(PC_CW1 + j),
                                                             in1=Y, op0=ALU.mult, op1=ALU.add),
                     reads=[bV, bY, b_prm], writes=[bY])
                P.op("dve", lambda v: v.scalar_tensor_tensor(out=Y, in0=V[:, HALO - 2:TT - 2], scalar=pcol(PC_CW0 + j),
                                                             in1=Y, op0=ALU.mult, op1=ALU.add),
                     reads=[bV, bY, b_prm], writes=[bY])
                bk, _ = win_unit(3 * NPC + j, False)
                P.op("dve", lambda v, bk=bk: v.tensor_tensor(out=Y, in0=ps[bk][:, 0:T], in1=Y, op=ALU.mult),
                     reads=[b_bank[bk], bY], writes=[bY])
                bk, _ = win_unit(5 * NPC + j, False)
                P.op("act", lambda a, bk=bk: a.activation(out=Z, in_=ps[bk][:, 0:T], func=AF.Silu),
                     reads=[b_bank[bk]], writes=[bZ])
                P.op("dve", lambda v: v.tensor_tensor(out=YS[NPC + j], in0=Y, in1=Z, op=ALU.mult),
                     reads=[bY, bZ], writes=[b_ys[NPC + j]])

            for g in range(4):
                w = WINDOWS[g]
                L = g + 1
                pre = win_units_interleaved([0, 1, 2, 3]) if (first and g == 0) else None
                for jj in range(4):
                    j = 4 * g + jj
                    ui = 4 + j % 2
                    U, bU = W528[ui], b_W528[ui]
                    su = saveU[:, j * 16:(j + 1) * 16]
                    if pre is not None:
                        bk, hs = pre[jj]
                    else:
                        bk, hs = win_unit(j, first)
                    if first:
                        P.op("act", lambda a, bk=bk, U=U: a.activation(out=U[:, 0:HW], in_=ps[bk][:, 0:HW],
                                                                      func=AF.Copy),
                             reads=[b_bank[bk]], writes=[bU])
                        P.op("act", lambda a, hs=hs, U=U: a.activation(out=U[:, HW:TT], in_=ps[hs][:, 0:HW],
                                                                      func=AF.Copy),
                             reads=[b_bank[hs]], writes=[bU])
                        P.op("act", lambda a, U=U, su=su: a.activation(out=su, in_=U[:, T:TT], func=AF.Copy),
                             reads=[bU], writes=[b_saveU[j]])
                    else:
                        P.op("act", lambda a, bk=bk, U=U: a.activation(out=U[:, HALO:TT], in_=ps[bk][:, 0:T],
                                                                      func=AF.Copy),
                             reads=[b_bank[bk]], writes=[bU])
                        P.op("act", lambda a, U=U, su=su: a.activation(out=U[:, 0:HALO], in_=su, func=AF.Copy),
                             reads=[b_saveU[j]], writes=[bU])
                    src, bsrc = U, bU
                    for l in range(1, L + 1):
                        lo = 2 ** l - 1
                        sh = 2 ** (l - 1)
                        di = l % 2
                        dst, bdst = P0t[di], b_P0[di]
                        P.op("dve", lambda v, dst=dst, src=src, lo=lo, sh=sh: v.tensor_tensor(
                            out=dst[:, lo:TT], in0=src[:, lo:TT], in1=src[:, lo - sh:TT - sh], op=ALU.add),
                            reads=[bsrc], writes=[bdst])
                        src, bsrc = dst, bdst
                    pl = pooled[:, jj * T:(jj + 1) * T]
                    P.op("dve", lambda v, pl=pl, src=src, U=U, w=w: v.scalar_tensor_tensor(
                        out=pl, in0=src[:, HALO:TT], scalar=1.0 / w, in1=U[:, HALO:TT], op0=ALU.mult,
                        op1=ALU.subtract), reads=[bsrc, bU], writes=[b_pooled[jj]])
                    io = (hf * 4 + g) * 16
                    P.op("dve", lambda v, src=src, io=io: v.tensor_tensor(
                        out=tmp16[:], in0=src[:, HALO:HALO + 16], in1=invc[:, io:io + 16], op=ALU.mult),
                        reads=[bsrc, b_invc], writes=[b_tmp16])
                    P.op("dve", lambda v, pl=pl, U=U: v.tensor_tensor(
                        out=pl[:, 0:16], in0=tmp16[:], in1=U[:, HALO:HALO + 16], op=ALU.subtract),
                        reads=[b_tmp16, bU], writes=[b_pooled[jj]])
                for jj in range(4):
                    j = 4 * g + jj
                    zi = 4 + jj
                    bk, _ = win_unit(NPC + j, False)
                    P.op("act", lambda a, bk=bk, zi=zi: a.activation(out=W512[zi], in_=ps[bk][:, 0:T], func=AF.Silu),
                         reads=[b_bank[bk]], writes=[b_W512[zi]])
                conv_round(4 * g)
                s = load_unit(pw_d[g], 2048)
                p_rhs = [pooled[:, kc * T:(kc + 1) * T] for kc in range(4)]
                for i in range(4):
                    j = 4 * g + i
                    bk, _ = mm_unit(s, 4, p_rhs, b_pooled, woff=i * 512)
                    P.op("dve", lambda v, bk=bk, j=j, i=i: v.scalar_tensor_tensor(
                        out=YS[j], in0=ps[bk][:, 0:T], scalar=pcol(PC_PSCALE + j), in1=W512[4 + i],
                        op0=ALU.mult, op1=ALU.mult),
                        reads=[b_bank[bk], b_W512[4 + i], b_prm], writes=[b_ys[j]])
                conv_round(4 * g + 1)
                conv_round(4 * g + 2)
                conv_round(4 * g + 3)

            ys_lo = [YS[k] for k in range(NPC)]
            ys_hi = [YS[NPC + k] for k in range(NPC)]
            nxt = phase0(hf + 1, False) if hf + 1 < NHALF else None

            def preload_xn(groups, hf=hf):
                for eg in groups:
                    e0 = eg * 4
                    src = xTm_d[hf][:, e0 * T:(e0 + 4) * T]
                    dst = AM[:, e0 * T:(e0 + 4) * T]
                    P.op("sp", lambda q, src=src, dst=dst: q.dma_start(out=dst, in_=src),
                         writes=b_xn[e0:e0 + 4], dma_sem=f"xr{eg}")

            preload_xn(range(4, 8))

            def advance(n=1, nxt=nxt):
                if nxt is None:
                    return
                for _ in range(n):
                    try:
                        next(nxt)
                    except StopIteration:
                        return

            for m in range(NCH):
                g0i, g1i = (m % 2), 2 + (m % 2)
                G0, G1 = S3[g0i], S3[g1i]
                bG0, bG1 = b_S3[g0i], b_S3[g1i]
                bk, _ = win_unit(6 * NPC + m, False)
                P.op("act", lambda a, bk=bk, G0=G0, m=m: a.activation(out=G0, in_=ps[bk][:, 0:T], func=AF.Sigmoid,
                                                                     bias=pcol(PC_GB0 + m)),
                     reads=[b_bank[bk], b_prm], writes=[bG0])
                bk, _ = win_unit(6 * NPC + NCH + m, False)
                P.op("act", lambda a, bk=bk, G1=G1, m=m: a.activation(out=G1, in_=ps[bk][:, 0:T], func=AF.Sigmoid,
                                                                     bias=pcol(PC_GB1 + m)),
                     reads=[b_bank[bk], b_prm], writes=[bG1])
                s = load_unit(wbr_d[0, m], 2048)
                bk, _ = mm_unit(s, NPC, ys_lo, b_ys[:NPC])
                P.op("dve", lambda v, bk=bk, G0=G0: v.tensor_tensor(out=G0, in0=ps[bk][:, 0:T], in1=G0, op=ALU.mult),
                     reads=[b_bank[bk], bG0], writes=[bG0])
                s = load_unit(wbr_d[1, m], 2048)
                bk, _ = mm_unit(s, NPC, ys_hi, b_ys[NPC:])
                P.op("dve", lambda v, bk=bk, G1=G1: v.tensor_tensor(out=G1, in0=ps[bk][:, 0:T], in1=G1, op=ALU.mult),
                     reads=[b_bank[bk], bG1], writes=[bG1])
                P.op("dve", lambda v, G0=G0, G1=G1, m=m: v.tensor_tensor(out=MG[m], in0=G0, in1=G1, op=ALU.add),
                     reads=[bG0, bG1], writes=[b_mg[m]])
                advance(1)
            advance(1)

            preload_xn(range(0, 4))
            mg_rhs = [MG[k] for k in range(NCH)]
            SQ3, bSQ3 = [S3[0], S3[1]], [b_S3[0], b_S3[1]]
            ACC3, bACC3 = S3[2], b_S3[2]
            RS3, bRS3 = S3[3], b_S3[3]

            for ei, e in enumerate(list(range(16, 32)) + list(range(0, 16))):
                s = load_unit(wo_d[e], 4096)
                bk, _ = mm_unit(s, NCH, mg_rhs, b_mg)
                P.op("dve", lambda v, bk=bk, e=e: v.tensor_tensor(out=XN[e], in0=ps[bk][:, 0:T], in1=XN[e],
                                                                  op=ALU.add),
                     reads=[b_bank[bk], b_xn[e]], writes=[b_xn[e]])
                if ei == 0:
                    P.op("act", lambda a, e=e: a.activation(out=ACC3, in_=XN[e], func=AF.Square),
                         reads=[b_xn[e]], writes=[bACC3])
                else:
                    q = e % 2
                    P.op("act", lambda a, e=e, q=q: a.activation(out=SQ3[q], in_=XN[e], func=AF.Square),
                         reads=[b_xn[e]], writes=[bSQ3[q]])
                    P.op("dve", lambda v, q=q: v.tensor_tensor(out=ACC3, in0=ACC3, in1=SQ3[q], op=ALU.add),
                         reads=[bACC3, bSQ3[q]], writes=[bACC3])
                advance(1)
            advance(100)

            def fn(t):
                return t.matmul(ps[7][:, 0:T], ones[:], ACC3, start=True, stop=True)
            P.op("pe", fn, reads=[b_ones, bACC3], writes=[b_bank[7]])
            P.op("dve", lambda v: v.tensor_scalar(out=RS3, in0=ps[7][:, 0:T], scalar1=1.0 / D, scalar2=EPS,
                                                 op0=ALU.mult, op1=ALU.add), reads=[b_bank[7]], writes=[bRS3])
            P.op("act", lambda a: a.activation(out=ACC3, in_=RS3, func=AF.Sqrt), reads=[bRS3], writes=[bACC3])
            P.op("dve", lambda v: v.reciprocal(out=RS3, in_=ACC3), reads=[bACC3], writes=[bRS3])
            last = (hf == NHALF - 1)
            for og in (4, 5, 6, 7, 0, 1, 2, 3):
                for e in range(og * 4, (og + 1) * 4):
                    if False and last and (e % 8) in (1, 4, 6):
                        P.op("act", lambda a, e=e: a.activation(out=XN[e], in_=XN[e], func=AF.Copy,
                                                                scale=pcol(PC_FNW + e)),
                             reads=[b_xn[e], b_prm], writes=[b_xn[e]])
                        P.op("pool", lambda g, e=e: g.tensor_tensor(out=XN[e], in0=XN[e], in1=RS3, op=ALU.mult),
                             reads=[b_xn[e], bRS3], writes=[b_xn[e]])
                    else:
                        P.op("dve", lambda v, e=e: v.scalar_tensor_tensor(out=XN[e], in0=XN[e],
                                                                          scalar=pcol(PC_FNW + e), in1=RS3,
                                                                          op0=ALU.mult, op1=ALU.mult),
                             reads=[b_xn[e], bRS3, b_prm], writes=[b_xn[e]])
                lo, hi = og * 4 * T, (og + 1) * 4 * T
                sig = P.op("sp", lambda q, lo=lo, hi=hi, hf=hf: q.dma_start(out=out_d[hf][:, lo:hi],
                                                                            in_=AM[:, lo:hi]),
                           reads=b_xn[og * 4:(og + 1) * 4], dma_sem=f"o{og}")
                out_sigs.append(sig)

        final_waits = {}
        for key, val in out_sigs:
            final_waits[key] = max(final_waits.get(key, 0), val)
        P.emit(block, sems, list(final_waits.items()))
    return nc


def _prep_inputs(x, norm_w, w_in, pool_w, pool_scale, conv_w, conv_b, gate_b, w_branch, w_out, final_norm_w):
    f = np.float32
    x = np.asarray(x, f)
    B, S, _ = x.shape
    xf = x.reshape(B * S, D)
    per_core = (B * S) // NCORES

    def colmat(v, n):
        return np.asarray(v, f).reshape(n, 128).T

    prm = np.zeros((128, NPRM), f)
    prm[:, PC_NORMW:PC_NORMW + 32] = colmat(norm_w[0], 32)
    prm[:, PC_PSCALE:PC_PSCALE + 16] = colmat(pool_scale[0], 16)
    prm[:, PC_CW0:PC_CW0 + 16] = colmat(conv_w[0, 0], 16)
    prm[:, PC_CW1:PC_CW1 + 16] = colmat(conv_w[0, 1], 16)
    prm[:, PC_CW2:PC_CW2 + 16] = colmat(conv_w[0, 2], 16)
    prm[:, PC_CB:PC_CB + 16] = colmat(conv_b[0], 16)
    prm[:, PC_GB0:PC_GB0 + 32] = colmat(gate_b[0, 0], 32)
    prm[:, PC_GB1:PC_GB1 + 32] = colmat(gate_b[0, 1], 32)
    prm[:, PC_FNW:PC_FNW + 32] = colmat(final_norm_w, 32)

    win = np.ascontiguousarray(
        np.asarray(w_in[0], f).reshape(32, 128, 160, 128).transpose(2, 1, 0, 3)).reshape(160, 128, 4096)
    pw = np.ascontiguousarray(
        np.asarray(pool_w[0], f).reshape(4, 4, 128, 4, 128).transpose(0, 2, 3, 1, 4)).reshape(4, 128, 2048)
    wbr = np.ascontiguousarray(
        np.asarray(w_branch[0], f).reshape(2, 16, 128, 32, 128).transpose(0, 3, 2, 1, 4)).reshape(2, 32, 128, 2048)
    wo = np.ascontiguousarray(
        np.asarray(w_out[0], f).reshape(32, 128, 32, 128).transpose(2, 1, 0, 3)).reshape(32, 128, 4096)

    in_maps = []
    for c in range(NCORES):
        xT = np.zeros((NHALF, 128, NCH * TT), f)
        xTm = np.zeros((NHALF, 128, NCH * T), f)
        pos = np.zeros((128, NHALF * 16), f)
        for hf in range(NHALF):
            t0 = c * per_core + hf * T
            s0 = t0 % S
            blk = np.zeros((TT, D), f)
            blk[HALO:] = xf[t0:t0 + T]
            if s0 > 0:
                blk[:HALO] = xf[t0 - HALO:t0]
            xT[hf] = blk.T.reshape(NCH, 128, TT).transpose(1, 0, 2).reshape(128, NCH * TT)
            xTm[hf] = blk[HALO:].T.reshape(NCH, 128, T).transpose(1, 0, 2).reshape(128, NCH * T)
            pos[:, hf * 16:(hf + 1) * 16] = (s0 + 1 + np.arange(16, dtype=f))[None, :]
        in_maps.append({"xT": xT, "xTm": xTm, "prm": prm, "pos": pos, "win": win, "pw": pw, "wbr": wbr, "wo": wo})
    return in_maps, (B, S)


_NC_CACHE = {}


def kernel(x, norm_w, w_in, pool_w, pool_scale, conv_w, conv_b, gate_b, w_branch, w_out, final_norm_w):
    in_maps, (B, S) = _prep_inputs(x, norm_w, w_in, pool_w, pool_scale, conv_w, conv_b, gate_b, w_branch, w_out,
                                   final_norm_w)
    if "nc" not in _NC_CACHE:
        _NC_CACHE["nc"] = build_program()
    nc = _NC_CACHE["nc"]
    res = run_bass_kernel_spmd(nc, in_maps, core_ids=list(range(NCORES)))
    outs = []
    for c in range(NCORES):
        o = np.asarray(res.results[c]["outT"]).reshape(NHALF, 128, NCH, T)
        outs.append(o.transpose(0, 3, 2, 1).reshape(NHALF * T, D))
    return np.concatenate(outs, axis=0).reshape(B, S, D).astype(np.float32)
```

```python
import contextlib
import numpy as np
import concourse.bass as bass
import concourse.mybir as mybir
from concourse.bass_utils import run_bass_kernel_spmd

F32 = mybir.dt.float32
BF16 = mybir.dt.bfloat16
AF = mybir.ActivationFunctionType
ALU = mybir.AluOpType

NCORES = 8
D = 4096
NCH = D // 128
PWD = 2048
NPC = PWD // 128
T = 512
HALO = 16
TT = T + HALO
HW = TT // 2
NHALF = 2
NSLOT = 5
NXS = 5
NP0 = NXS + 5
EPS = 1e-6
WINDOWS = (2, 4, 8, 16)

PC_NORMW = 0
PC_PSCALE = 32
PC_CW0 = 48
PC_CW1 = 64
PC_CW2 = 80
PC_CB = 96
PC_GB0 = 112
PC_GB1 = 144
PC_FNW = 176
NPRM = 208

COMPUTE = ("pe", "act", "dve", "pool")
ENGS = ("pe", "act", "dve", "sp", "pool")


class Region:
    def __init__(self, name):
        self.name = name
        self.bufs = []


class Buf:
    def __init__(self, name, region=None, lo=0, hi=0):
        self.name = name
        self.w = None
        self.r = []
        self.lo, self.hi = lo, hi
        self.aliases = []
        if region is not None:
            for o in region.bufs:
                if o.lo < hi and lo < o.hi:
                    o.aliases.append(self)
                    self.aliases.append(o)
            region.bufs.append(self)


class Prog:
    def __init__(self):
        self.ops = {e: [] for e in ENGS}
        self.seq = {e: 0 for e in COMPUTE}
        self.waited = {e: {} for e in ENGS}
        self.needed = {e: set() for e in COMPUTE}
        self.dma_cnt = {}

    def op(self, eng, fn, reads=(), writes=(), dma_sem=None):
        deps = {}

        def add(sig, same_ok):
            if sig is None:
                return
            key, val = sig
            if key == eng and not same_ok:
                return
            if deps.get(key, 0) < val:
                deps[key] = val

        wset = []
        for b in writes:
            wset.append(b)
            wset.extend(b.aliases)
        for b in reads:
            add(b.w, True)
            for o in b.aliases:
                add(o.w, True)
        for b in wset:
            add(b.w, eng != "pe")
            for s in b.r:
                add(s, False)
        waits = []
        wd = self.waited[eng]
        for key, val in deps.items():
            if wd.get(key, 0) >= val:
                continue
            wd[key] = val
            waits.append((key, val))
            if key in COMPUTE:
                self.needed[key].add(val)
        if dma_sem is None:
            self.seq[eng] += 1
            sig = (eng, self.seq[eng])
        else:
            self.dma_cnt[dma_sem] = self.dma_cnt.get(dma_sem, 0) + 16
            sig = (dma_sem, self.dma_cnt[dma_sem])
        self.ops[eng].append((waits, fn, sig))
        for b in reads:
            b.r.append(sig)
        for b in wset:
            b.w = sig
            b.r = []
        return sig

    def emit(self, block, sems, final_waits):
        ranks = {}
        for e in COMPUTE:
            ranks[e] = {v: i + 1 for i, v in enumerate(sorted(self.needed[e]))}

        def tr(key, val):
            if key in COMPUTE:
                return ranks[key][val]
            return val

        def make_body(eng):
            ops = self.ops[eng]

            def body(e):
                for waits, fn, sig in ops:
                    for key, val in waits:
                        e.wait_ge(sems[key], tr(key, val))
                    ins = fn(e)
                    if sig[0] in COMPUTE:
                        if sig[1] in ranks[eng]:
                            ins.then_inc(sems[eng], 1)
                    else:
                        ins.then_inc(sems[sig[0]], 16)
                if eng == "sp":
                    for key, val in final_waits:
                        e.wait_ge(sems[key], tr(key, val))
            return body

        block.tensor(make_body("pe"))
        block.scalar(make_body("act"))
        block.vector(make_body("dve"))
        block.sync(make_body("sp"))
        block.gpsimd(make_body("pool"))


def build_program():
    nc = bass.Bass("TRN2", target_bir_lowering=False)
    xT_d = nc.dram_tensor("xT", [NHALF, 128, NCH * TT], F32, kind="ExternalInput").ap()
    xTm_d = nc.dram_tensor("xTm", [NHALF, 128, NCH * T], F32, kind="ExternalInput").ap()
    prm_d = nc.dram_tensor("prm", [128, NPRM], F32, kind="ExternalInput").ap()
    pos_d = nc.dram_tensor("pos", [128, NHALF * 16], F32, kind="ExternalInput").ap()
    win_d = nc.dram_tensor("win", [160, 128, 4096], F32, kind="ExternalInput").ap()
    pw_d = nc.dram_tensor("pw", [4, 128, 2048], F32, kind="ExternalInput").ap()
    wbr_d = nc.dram_tensor("wbr", [2, 32, 128, 2048], F32, kind="ExternalInput").ap()
    wo_d = nc.dram_tensor("wo", [32, 128, 4096], F32, kind="ExternalInput").ap()
    out_d = nc.dram_tensor("outT", [NHALF, 128, NCH * T], F32, kind="ExternalOutput").ap()

    P = Prog()
    es = contextlib.ExitStack()
    with es:
        AM = es.enter_context(nc.sbuf_tensor("AM", [128, 24576], F32))
        Hh = es.enter_context(nc.sbuf_tensor("Hh", [128, 8448], F32))
        P0 = es.enter_context(nc.sbuf_tensor("P0", [128, NP0 * TT], F32))
        S3t = es.enter_context(nc.sbuf_tensor("S3t", [128, 4 * T], F32))
        wring = [es.enter_context(nc.sbuf_tensor(f"wr{i}", [128, 4096], BF16)) for i in range(NSLOT)]
        pooled = es.enter_context(nc.sbuf_tensor("pooled", [128, 4 * T], BF16))
        prm = es.enter_context(nc.sbuf_tensor("prm_sb", [128, NPRM], F32))
        pos = es.enter_context(nc.sbuf_tensor("pos_sb", [128, NHALF * 16], F32))
        invc = es.enter_context(nc.sbuf_tensor("invc", [128, NHALF * 4 * 16], F32))
        ones = es.enter_context(nc.sbuf_tensor("ones", [128, 128], F32))
        tmp16 = es.enter_context(nc.sbuf_tensor("tmp16", [128, 16], F32))
        saveU = es.enter_context(nc.sbuf_tensor("saveU", [128, NPC * 16], F32))
        saveV = es.enter_context(nc.sbuf_tensor("saveV", [128, NPC * 16], F32))
        ps = [es.enter_context(nc.psum_tensor(f"ps{i}", [128, 512], F32)) for i in range(8)]

        sem_names = list(COMPUTE) + [f"w{i}" for i in range(NSLOT)] + [f"xl{i}" for i in range(8)] + \
            [f"xs{i}" for i in range(NXS)] + [f"xr{i}" for i in range(8)] + [f"o{i}" for i in range(8)] + \
            ["prm", "pos"]
        sems = {n: es.enter_context(nc.semaphore(n)) for n in sem_names}
        block = es.enter_context(nc.Block())

        rAM, rH, rP0, rS3 = Region("AM"), Region("H"), Region("P0"), Region("S3")

        def f32view(t, blo, n):
            return t[:, blo // 4:blo // 4 + n]

        def bf16view(t, blo, n):
            return t[:, blo // 4:(blo + 2 * n) // 4].bitcast(BF16)

        XN = [f32view(AM, e * 2048, T) for e in range(NCH)]
        b_xn = [Buf(f"xn{e}", rAM, e * 2048, (e + 1) * 2048) for e in range(NCH)]
        YS = [bf16view(AM, j * 1024, T) for j in range(NCH)]
        b_ys = [Buf(f"ys{j}", rAM, j * 1024, (j + 1) * 1024) for j in range(NCH)]
        SCR0 = 32768
        W528 = [f32view(AM, SCR0 + i * 2112, TT) for i in range(6)]
        b_W528 = [Buf(f"W528_{i}", rAM, SCR0 + i * 2112, SCR0 + (i + 1) * 2112) for i in range(6)]
        W512_0 = SCR0 + 6 * 2112
        W512 = [f32view(AM, W512_0 + i * 2048, T) for i in range(8)]
        b_W512 = [Buf(f"W512_{i}", rAM, W512_0 + i * 2048, W512_0 + (i + 1) * 2048) for i in range(8)]
        assert W512_0 + 8 * 2048 <= 65536
        MG0 = 65536
        MG = [bf16view(AM, MG0 + m * 1024, T) for m in range(NCH)]
        b_mg = [Buf(f"mg{m}", rAM, MG0 + m * 1024, MG0 + (m + 1) * 1024) for m in range(NCH)]
        XT0 = [f32view(AM, c * 2112, TT) for c in range(NCH)]
        b_xt0 = [Buf(f"xt0_{c}", rAM, c * 2112, (c + 1) * 2112) for c in range(NCH)]

        hT_all = Hh[:, :].bitcast(BF16)

        def hT(c, lo=0, hi=TT):
            return hT_all[:, c * TT + lo:c * TT + hi]
        b_hT = [Buf(f"hT{c}", rH, c * 1056, (c + 1) * 1056) for c in range(NCH)]

        P0t = [f32view(P0, i * 2112, TT) for i in range(NP0)]
        b_P0 = [Buf(f"P0_{i}", rP0, i * 2112, (i + 1) * 2112) for i in range(NP0)]
        S3 = [f32view(S3t, i * 2048, T) for i in range(4)]
        b_S3 = [Buf(f"S3_{i}", rS3, i * 2048, (i + 1) * 2048) for i in range(4)]

        b_slot = [Buf(f"slot{i}") for i in range(NSLOT)]
        b_pooled = [Buf(f"pooled{i}") for i in range(4)]
        b_prm, b_pos, b_invc, b_ones, b_tmp16 = Buf("prm"), Buf("pos"), Buf("invc"), Buf("ones"), Buf("tmp16")
        b_saveU = [Buf(f"saveU{j}") for j in range(NPC)]
        b_saveV = [Buf(f"saveV{j}") for j in range(NPC)]
        b_bank = [Buf(f"bank{i}") for i in range(8)]

        def pcol(c):
            return prm[:, c:c + 1]

        state = {"bank": 0, "hs": 0, "unit": 0}

        def next_bank():
            i = state["bank"]
            state["bank"] = (i + 1) % 6
            return i

        def next_hs():
            i = state["hs"]
            state["hs"] = (i + 1) % 2
            return 6 + i

        def load_unit(src_ap, ncols):
            s = state["unit"] % NSLOT
            state["unit"] += 1
            dst = wring[s][:, 0:ncols]
            rd = list(b_xt0[:16]) if state["unit"] == 1 else []
            P.op("pool", lambda g, dst=dst, src_ap=src_ap: g.dma_start(out=dst, in_=src_ap, max_dma_last_dim=8192),
                 reads=rd, writes=[b_slot[s]], dma_sem=f"w{s}")
            return s

        def mm_unit(s, K, rhs_main, rhs_bufs, halo_rhs=None, woff=0, split=False, nmain=T, nhalo=HALO):
            bk = next_bank()
            hs = next_hs() if halo_rhs is not None else None
            wt = wring[s]
            bank_ap = ps[bk][:, 0:nmain]
            hs_ap = ps[hs][:, 0:nhalo] if hs is not None else None
            if split:
                for k in range(K):
                    def fnk(t, k=k):
                        w_ap = wt[:, woff + k * 128:woff + (k + 1) * 128]
                        ins = t.matmul(bank_ap, w_ap, rhs_main[k], start=(k == 0), stop=(k == K - 1))
                        if hs_ap is not None:
                            ins = t.matmul(hs_ap, w_ap, halo_rhs[k], start=(k == 0), stop=(k == K - 1))
                        return ins
                    P.op("pe", fnk, reads=[b_slot[s], rhs_bufs[k]],
                         writes=[b_bank[bk]] + ([b_bank[hs]] if hs is not None else []))
                return bk, hs

            def fn(t):
                ins = None
                for k in range(K):
                    w_ap = wt[:, woff + k * 128:woff + (k + 1) * 128]
                    ins = t.matmul(bank_ap, w_ap, rhs_main[k], start=(k == 0), stop=(k == K - 1))
                    if hs_ap is not None:
                        ins = t.matmul(hs_ap, w_ap, halo_rhs[k], start=(k == 0), stop=(k == K - 1))
                return ins

            writes = [b_bank[bk]] + ([b_bank[hs]] if hs is not None else [])
            P.op("pe", fn, reads=[b_slot[s]] + list(rhs_bufs), writes=writes)
            return bk, hs

        P.op("sp", lambda e: e.dma_start(out=prm[:], in_=prm_d), writes=[b_prm], dma_sem="prm")
        P.op("sp", lambda e: e.dma_start(out=pos[:], in_=pos_d), writes=[b_pos], dma_sem="pos")
        P.op("dve", lambda v: v.memset(ones[:], 1.0), writes=[b_ones])
        for hf in range(NHALF):
            for g, w in enumerate(WINDOWS):
                o = (hf * 4 + g) * 16
                P.op("dve", lambda v, o=o, hf=hf, w=w: v.tensor_scalar(
                    out=invc[:, o:o + 16], in0=pos[:, hf * 16:(hf + 1) * 16], scalar1=float(w), scalar2=None,
                    op0=ALU.min), reads=[b_pos], writes=[b_invc])
        P.op("dve", lambda v: v.reciprocal(out=invc[:], in_=invc[:]), reads=[b_invc], writes=[b_invc])

        def phase0(hf, resident):
            SQ, bSQ = [P0t[NXS], P0t[NXS + 1]], [b_P0[NXS], b_P0[NXS + 1]]
            ACC, bACC = P0t[NXS + 2], b_P0[NXS + 2]
            RSA, bRSA = P0t[NXS + 3], b_P0[NXS + 3]
            RSB, bRSB = P0t[NXS + 4], b_P0[NXS + 4]
            if resident:
                for cg in range(4):
                    lo, hi = cg * 8 * TT, (cg + 1) * 8 * TT
                    P.op("sp",
                         lambda e, lo=lo, hi=hi: e.dma_start(out=AM[:, lo:hi], in_=xT_d[hf][:, lo:hi]),
                         writes=b_xt0[cg * 8:(cg + 1) * 8], dma_sem=f"xl{cg}")
            PF = NXS - 1
            scnt = [0]

            def issue_load(c):
                i = scnt[0] % NXS
                scnt[0] += 1
                P.op("sp", lambda e, i=i, c=c: e.dma_start(out=P0t[i], in_=xT_d[hf][:, c * TT:(c + 1) * TT]),
                     writes=[b_P0[i]], dma_sem=f"xs{i}")
                return i

            pend = []

            def start_pass():
                if not resident:
                    for c in range(PF):
                        pend.append(issue_load(c))

            def get_x(c):
                if resident:
                    return XT0[c], b_xt0[c]
                if c + PF < NCH:
                    pend.append(issue_load(c + PF))
                i = pend.pop(0)
                return P0t[i], b_P0[i]

            start_pass()
            for c in range(NCH):
                xt, bxt = get_x(c)
                if c == 0:
                    P.op("act", lambda a, xt=xt: a.activation(out=ACC, in_=xt, func=AF.Square),
                         reads=[bxt], writes=[bACC])
                else:
                    q = c % 2
                    P.op("act", lambda a, xt=xt, q=q: a.activation(out=SQ[q], in_=xt, func=AF.Square),
                         reads=[bxt], writes=[bSQ[q]])
                    P.op("dve", lambda v, q=q: v.tensor_tensor(out=ACC, in0=ACC, in1=SQ[q], op=ALU.add),
                         reads=[bACC, bSQ[q]], writes=[bACC])
                yield
            bk0, hb0 = next_bank(), next_hs()

            def fn(t):
                t.matmul(ps[bk0][:, 0:T], ones[:], ACC[:, HALO:TT], start=True, stop=True)
                return t.matmul(ps[hb0][:, 0:HALO], ones[:], ACC[:, 0:HALO], start=True, stop=True)
            P.op("pe", fn, reads=[b_ones, bACC], writes=[b_bank[bk0], b_bank[hb0]])
            P.op("dve", lambda v: v.tensor_scalar(out=RSA[:, HALO:TT], in0=ps[bk0][:, 0:T], scalar1=1.0 / D,
                                                 scalar2=EPS, op0=ALU.mult, op1=ALU.add),
                 reads=[b_bank[bk0]], writes=[bRSA])
            P.op("dve", lambda v: v.tensor_scalar(out=RSA[:, 0:HALO], in0=ps[hb0][:, 0:HALO], scalar1=1.0 / D,
                                                 scalar2=EPS, op0=ALU.mult, op1=ALU.add),
                 reads=[b_bank[hb0]], writes=[bRSA])
            P.op("act", lambda a: a.activation(out=RSB, in_=RSA, func=AF.Sqrt), reads=[bRSA], writes=[bRSB])
            P.op("dve", lambda v: v.reciprocal(out=RSA, in_=RSB), reads=[bRSB], writes=[bRSA])
            start_pass()
            yield
            for c in range(NCH):
                xt, bxt = get_x(c)
                P.op("dve", lambda v, c=c, xt=xt: v.scalar_tensor_tensor(
                    out=hT(c), in0=xt, scalar=pcol(PC_NORMW + c), in1=RSA, op0=ALU.mult, op1=ALU.mult),
                    reads=[bxt, bRSA, b_prm], writes=[b_hT[c]])
                yield


        out_sigs = []
        for _ in phase0(0, True):
            pass
        for hf in range(NHALF):
            first = (hf == 0)
            h_main = [hT(c, HALO, TT) for c in range(NCH)]
            h_lo = [hT(c, 0, HW) for c in range(NCH)]
            h_hi = [hT(c, HW, TT) for c in range(NCH)]

            def win_unit(col_chunk, halo, split=False):
                s = load_unit(win_d[col_chunk], 4096)
                if halo:
                    return mm_unit(s, NCH, h_lo, b_hT, halo_rhs=h_hi, split=split, nmain=HW, nhalo=HW)
                return mm_unit(s, NCH, h_main, b_hT, split=split)

            def win_units_interleaved(col_chunks):
                n = len(col_chunks)
                slots = [load_unit(win_d[cc], 4096) for cc in col_chunks]
                banks = []
                for i in range(n):
                    if i < 2:
                        banks.append((next_bank(), next_hs()))
                    else:
                        banks.append((next_bank(), next_bank()))
                for k in range(NCH):
                    for i in range(n):
                        wt = wring[slots[i]]
                        bk, hs = banks[i]

                        def fnk(t, k=k, wt=wt, bk=bk, hs=hs):
                            w_ap = wt[:, k * 128:(k + 1) * 128]
                            t.matmul(ps[bk][:, 0:HW], w_ap, h_lo[k], start=(k == 0), stop=(k == NCH - 1))
                            return t.matmul(ps[hs][:, 0:HW], w_ap, h_hi[k], start=(k == 0), stop=(k == NCH - 1))
                        P.op("pe", fnk, reads=[b_slot[slots[i]], b_hT[k]], writes=[b_bank[bk], b_bank[hs]])
                return banks

            def conv_round(j, hf=hf, first=first):
                ui, vi, yi, zi = j % 2, 2 + j % 2, j % 2, 2 + j % 2
                U, V, Y, Z = W528[ui], W528[vi], W512[yi], W512[zi]
                bU, bV, bY, bZ = b_W528[ui], b_W528[vi], b_W512[yi], b_W512[zi]
                sv = saveV[:, j * 16:(j + 1) * 16]
                bk, hs = win_unit(2 * NPC + j, first)
                if first:
                    P.op("act", lambda a, bk=bk: a.activation(out=U[:, 0:HW], in_=ps[bk][:, 0:HW], func=AF.Copy),
                         reads=[b_bank[bk]], writes=[bU])
                    P.op("act", lambda a, hs=hs: a.activation(out=U[:, HW:TT], in_=ps[hs][:, 0:HW], func=AF.Copy),
                         reads=[b_bank[hs]], writes=[bU])
                else:
                    P.op("act", lambda a, bk=bk: a.activation(out=U[:, HALO:TT], in_=ps[bk][:, 0:T], func=AF.Copy),
                         reads=[b_bank[bk]], writes=[bU])
                bk, hs = win_unit(4 * NPC + j, first)
                if first:
                    P.op("dve", lambda v, bk=bk: v.tensor_tensor(out=V[:, 0:HW], in0=ps[bk][:, 0:HW], in1=U[:, 0:HW],
                                                                op=ALU.mult), reads=[b_bank[bk], bU], writes=[bV])
                    P.op("dve", lambda v, hs=hs: v.tensor_tensor(out=V[:, HW:TT], in0=ps[hs][:, 0:HW],
                                                                in1=U[:, HW:TT], op=ALU.mult),
                         reads=[b_bank[hs], bU], writes=[bV])
                    P.op("act", lambda a: a.activation(out=sv, in_=V[:, T:TT], func=AF.Copy),
                         reads=[bV], writes=[b_saveV[j]])
                else:
                    P.op("dve", lambda v, bk=bk: v.tensor_tensor(out=V[:, HALO:TT], in0=ps[bk][:, 0:T],
                                                                in1=U[:, HALO:TT], op=ALU.mult),
                         reads=[b_bank[bk], bU], writes=[bV])
                    P.op("dve", lambda v: v.tensor_copy(out=V[:, 0:HALO], in_=sv), reads=[b_saveV[j]], writes=[bV])
                P.op("act", lambda a: a.activation(out=Y, in_=V[:, HALO:TT], func=AF.Identity,
                                                   bias=pcol(PC_CB + j), scale=pcol(PC_CW2 + j)),
                     reads=[bV, b_prm], writes=[bY])
                P.op("dve", lambda v: v.scalar_tensor_tensor(out=Y, in0=V[:, HALO - 1:TT - 1], scalar=pcol(PC_CW1 + j),
                                                             in1=Y, op0=ALU.mult, op1=ALU.add),
                     reads=[bV, bY, b_prm], writes=[bY])
                P.op("dve", lambda v: v.scalar_tensor_tensor(out=Y, in0=V[:, HALO - 2:TT - 2], scalar=pcol(PC_CW0 + j),
                                                             in1=Y, op0=ALU.mult, op1=ALU.add),
                     reads=[bV, bY, b_prm], writes=[bY])
                bk, _ = win_unit(3 * NPC + j, False)
                P.op("dve", lambda v, bk=bk: v.tensor_tensor(out=Y, in0=ps[bk][:, 0:T], in1=Y, op=ALU.mult),
                     reads=[b_bank[bk], bY], writes=[bY])
                bk, _ = win_unit(5 * NPC + j, False)
                P.op("act", lambda a, bk=bk: a.activation(out=Z, in_=ps[bk][:, 0:T], func=AF.Silu),
                     reads=[b_bank[bk]], writes=[bZ])
                P.op("dve", lambda v: v.tensor_tensor(out=YS[NPC + j], in0=Y, in1=Z, op=ALU.mult),
                     reads=[bY, bZ], writes=[b_ys[NPC + j]])

            for g in range(4):
                w = WINDOWS[g]
                L = g + 1
                pre = win_units_interleaved([0, 1, 2, 3]) if (first and g == 0) else None
                for jj in range(4):
                    j = 4 * g + jj
                    ui = 4 + j % 2
                    U, bU = W528[ui], b_W528[ui]
                    su = saveU[:, j * 16:(j + 1) * 16]
                    if pre is not None:
                        bk, hs = pre[jj]
                    else:
                        bk, hs = win_unit(j, first)
                    if first:
                        P.op("act", lambda a, bk=bk, U=U: a.activation(out=U[:, 0:HW], in_=ps[bk][:, 0:HW],
                                                                      func=AF.Copy),
                             reads=[b_bank[bk]], writes=[bU])
                        P.op("act", lambda a, hs=hs, U=U: a.activation(out=U[:, HW:TT], in_=ps[hs][:, 0:HW],
                                                                      func=AF.Copy),
                             reads=[b_bank[hs]], writes=[bU])
                        P.op("act", lambda a, U=U, su=su: a.activation(out=su, in_=U[:, T:TT], func=AF.Copy),
                             reads=[bU], writes=[b_saveU[j]])
                    else:
                        P.op("act", lambda a, bk=bk, U=U: a.activation(out=U[:, HALO:TT], in_=ps[bk][:, 0:T],
                                                                      func=AF.Copy),
                             reads=[b_bank[bk]], writes=[bU])
                        P.op("act", lambda a, U=U, su=su: a.activation(out=U[:, 0:HALO], in_=su, func=AF.Copy),
                             reads=[b_saveU[j]], writes=[bU])
                    src, bsrc = U, bU
                    for l in range(1, L + 1):
                        lo = 2 ** l - 1
                        sh = 2 ** (l - 1)
                        di = l % 2
                        dst, bdst = P0t[di], b_P0[di]
                        P.op("dve", lambda v, dst=dst, src=src, lo=lo, sh=sh: v.tensor_tensor(
                            out=dst[:, lo:TT], in0=src[:, lo:TT], in1=src[:, lo - sh:TT - sh], op=ALU.add),
                            reads=[bsrc], writes=[bdst])
                        src, bsrc = dst, bdst
                    pl = pooled[:, jj * T:(jj + 1) * T]
                    P.op("dve", lambda v, pl=pl, src=src, U=U, w=w: v.scalar_tensor_tensor(
                        out=pl, in0=src[:, HALO:TT], scalar=1.0 / w, in1=U[:, HALO:TT], op0=ALU.mult,
                        op1=ALU.subtract), reads=[bsrc, bU], writes=[b_pooled[jj]])
                    io = (hf * 4 + g) * 16
                    P.op("dve", lambda v, src=src, io=io: v.tensor_tensor(
                        out=tmp16[:], in0=src[:, HALO:HALO + 16], in1=invc[:, io:io + 16], op=ALU.mult),
                        reads=[bsrc, b_invc], writes=[b_tmp16])
                    P.op("dve", lambda v, pl=pl, U=U: v.tensor_tensor(
                        out=pl[:, 0:16], in0=tmp16[:], in1=U[:, HALO:HALO + 16], op=ALU.subtract),
                        reads=[b_tmp16, bU], writes=[b_pooled[jj]])
                for jj in range(4):
                    j = 4 * g + jj
                    zi = 4 + jj
                    bk, _ = win_unit(NPC + j, False)
                    P.op("act", lambda a, bk=bk, zi=zi: a.activation(out=W512[zi], in_=ps[bk][:, 0:T], func=AF.Silu),
                         reads=[b_bank[bk]], writes=[b_W512[zi]])
                conv_round(4 * g)
                s = load_unit(pw_d[g], 2048)
                p_rhs = [pooled[:, kc * T:(kc + 1) * T] for kc in range(4)]
                for i in range(4):
                    j = 4 * g + i
                    bk, _ = mm_unit(s, 4, p_rhs, b_pooled, woff=i * 512)
                    P.op("dve", lambda v, bk=bk, j=j, i=i: v.scalar_tensor_tensor(
                        out=YS[j], in0=ps[bk][:, 0:T], scalar=pcol(PC_PSCALE + j), in1=W512[4 + i],
                        op0=ALU.mult, op1=ALU.mult),
                        reads=[b_bank[bk], b_W512[4 + i], b_prm], writes=[b_ys[j]])
                conv_round(4 * g + 1)
                conv_round(4 * g + 2)
                conv_round(4 * g + 3)

            ys_lo = [YS[k] for k in range(NPC)]
            ys_hi = [YS[NPC + k] for k in range(NPC)]
            nxt = phase0(hf + 1, False) if hf + 1 < NHALF else None

            def preload_xn(groups, hf=hf):
                for eg in groups:
                    e0 = eg * 4
                    src = xTm_d[hf][:, e0 * T:(e0 + 4) * T]
                    dst = AM[:, e0 * T:(e0 + 4) * T]
                    P.op("sp", lambda q, src=src, dst=dst: q.dma_start(out=dst, in_=src),
                         writes=b_xn[e0:e0 + 4], dma_sem=f"xr{eg}")

            preload_xn(range(4, 8))

            def advance(n=1, nxt=nxt):
                if nxt is None:
                    return
                for _ in range(n):
                    try:
                        next(nxt)
                    except StopIteration:
                        return

            for m in range(NCH):
                g0i, g1i = (m % 2), 2 + (m % 2)
                G0, G1 = S3[g0i], S3[g1i]
                bG0, bG1 = b_S3[g0i], b_S3[g1i]
                bk, _ = win_unit(6 * NPC + m, False)
                P.op("act", lambda a, bk=bk, G0=G0, m=m: a.activation(out=G0, in_=ps[bk][:, 0:T], func=AF.Sigmoid,
                                                                     bias=pcol(PC_GB0 + m)),
                     reads=[b_bank[bk], b_prm], writes=[bG0])
                bk, _ = win_unit(6 * NPC + NCH + m, False)
                P.op("act", lambda a, bk=bk, G1=G1, m=m: a.activation(out=G1, in_=ps[bk][:, 0:T], func=AF.Sigmoid,
                                                                     bias=pcol(PC_GB1 + m)),
                     reads=[b_bank[bk], b_prm], writes=[bG1])
                s = load_unit(wbr_d[0, m], 2048)
                bk, _ = mm_unit(s, NPC, ys_lo, b_ys[:NPC])
                P.op("dve", lambda v, bk=bk, G0=G0: v.tensor_tensor(out=G0, in0=ps[bk][:, 0:T], in1=G0, op=ALU.mult),
                     reads=[b_bank[bk], bG0], writes=[bG0])
                s = load_unit(wbr_d[1, m], 2048)
                bk, _ = mm_unit(s, NPC, ys_hi, b_ys[NPC:])
                P.op("dve", lambda v, bk=bk, G1=G1: v.tensor_tensor(out=G1, in0=ps[bk][:, 0:T], in1=G1, op=ALU.mult),
                     reads=[b_bank[bk], bG1], writes=[bG1])
                P.op("dve", lambda v, G0=G0, G1=G1, m=m: v.tensor_tensor(out=MG[m], in0=G0, in1=G1, op=ALU.add),
                     reads=[bG0, bG1], writes=[b_mg[m]])
                advance(1)
            advance(1)

            preload_xn(range(0, 4))
            mg_rhs = [MG[k] for k in range(NCH)]
            SQ3, bSQ3 = [S3[0], S3[1]], [b_S3[0], b_S3[1]]
            ACC3, bACC3 = S3[2], b_S3[2]
            RS3, bRS3 = S3[3], b_S3[3]

            for ei, e in enumerate(list(range(16, 32)) + list(range(0, 16))):
                s = load_unit(wo_d[e], 4096)
                bk, _ = mm_unit(s, NCH, mg_rhs, b_mg)
                P.op("dve", lambda v, bk=bk, e=e: v.tensor_tensor(out=XN[e], in0=ps[bk][:, 0:T], in1=XN[e],
                                                                  op=ALU.add),
                     reads=[b_bank[bk], b_xn[e]], writes=[b_xn[e]])
                if ei == 0:
                    P.op("act", lambda a, e=e: a.activation(out=ACC3, in_=XN[e], func=AF.Square),
                         reads=[b_xn[e]], writes=[bACC3])
                else:
                    q = e % 2
                    P.op("act", lambda a, e=e, q=q: a.activation(out=SQ3[q], in_=XN[e], func=AF.Square),
                         reads=[b_xn[e]], writes=[bSQ3[q]])
                    P.op("dve", lambda v, q=q: v.tensor_tensor(out=ACC3, in0=ACC3, in1=SQ3[q], op=ALU.add),
                         reads=[bACC3, bSQ3[q]], writes=[bACC3])
                advance(1)
            advance(100)

            def fn(t):
                return t.matmul(ps[7][:, 0:T], ones[:], ACC3, start=True, stop=True)
            P.op("pe", fn, reads=[b_ones, bACC3], writes=[b_bank[7]])
            P.op("dve", lambda v: v.tensor_scalar(out=RS3, in0=ps[7][:, 0:T], scalar1=1.0 / D, scalar2=EPS,
                                                 op0=ALU.mult, op1=ALU.add), reads=[b_bank[7]], writes=[bRS3])
            P.op("act", lambda a: a.activation(out=ACC3, in_=RS3, func=AF.Sqrt), reads=[bRS3], writes=[bACC3])
            P.op("dve", lambda v: v.reciprocal(out=RS3, in_=ACC3), reads=[bACC3], writes=[bRS3])
            last = (hf == NHALF - 1)
            for og in (4, 5, 6, 7, 0, 1, 2, 3):
                for e in range(og * 4, (og + 1) * 4):
                    if False and last and (e % 8) in (1, 4, 6):
                        P.op("act", lambda a, e=e: a.activation(out=XN[e], in_=XN[e], func=AF.Copy,
                                                                scale=pcol(PC_FNW + e)),
                             reads=[b_xn[e], b_prm], writes=[b_xn[e]])
                        P.op("pool", lambda g, e=e: g.tensor_tensor(out=XN[e], in0=XN[e], in1=RS3, op=ALU.mult),
                             reads=[b_xn[e], bRS3], writes=[b_xn[e]])
                    else:
                        P.op("dve", lambda v, e=e: v.scalar_tensor_tensor(out=XN[e], in0=XN[e],
                                                                          scalar=pcol(PC_FNW + e), in1=RS3,
                                                                          op0=ALU.mult, op1=ALU.mult),
                             reads=[b_xn[e], bRS3, b_prm], writes=[b_xn[e]])
                lo, hi = og * 4 * T, (og + 1) * 4 * T
                sig = P.op("sp", lambda q, lo=lo, hi=hi, hf=hf: q.dma_start(out=out_d[hf][:, lo:hi],
                                                                            in_=AM[:, lo:hi]),
                           reads=b_xn[og * 4:(og + 1) * 4], dma_sem=f"o{og}")
                out_sigs.append(sig)

        final_waits = {}
        for key, val in out_sigs:
            final_waits[key] = max(final_waits.get(key, 0), val)
        P.emit(block, sems, list(final_waits.items()))
    return nc


def _prep_inputs(x, norm_w, w_in, pool_w, pool_scale, conv_w, conv_b, gate_b, w_branch, w_out, final_norm_w):
    f = np.float32
    x = np.asarray(x, f)
    B, S, _ = x.shape
    xf = x.reshape(B * S, D)
    per_core = (B * S) // NCORES

    def colmat(v, n):
        return np.asarray(v, f).reshape(n, 128).T

    prm = np.zeros((128, NPRM), f)
    prm[:, PC_NORMW:PC_NORMW + 32] = colmat(norm_w[0], 32)
    prm[:, PC_PSCALE:PC_PSCALE + 16] = colmat(pool_scale[0], 16)
    prm[:, PC_CW0:PC_CW0 + 16] = colmat(conv_w[0, 0], 16)
    prm[:, PC_CW1:PC_CW1 + 16] = colmat(conv_w[0, 1], 16)
    prm[:, PC_CW2:PC_CW2 + 16] = colmat(conv_w[0, 2], 16)
    prm[:, PC_CB:PC_CB + 16] = colmat(conv_b[0], 16)
    prm[:, PC_GB0:PC_GB0 + 32] = colmat(gate_b[0, 0], 32)
    prm[:, PC_GB1:PC_GB1 + 32] = colmat(gate_b[0, 1], 32)
    prm[:, PC_FNW:PC_FNW + 32] = colmat(final_norm_w, 32)

    win = np.ascontiguousarray(
        np.asarray(w_in[0], f).reshape(32, 128, 160, 128).transpose(2, 1, 0, 3)).reshape(160, 128, 4096)
    pw = np.ascontiguousarray(
        np.asarray(pool_w[0], f).reshape(4, 4, 128, 4, 128).transpose(0, 2, 3, 1, 4)).reshape(4, 128, 2048)
    wbr = np.ascontiguousarray(
        np.asarray(w_branch[0], f).reshape(2, 16, 128, 32, 128).transpose(0, 3, 2, 1, 4)).reshape(2, 32, 128, 2048)
    wo = np.ascontiguousarray(
        np.asarray(w_out[0], f).reshape(32, 128, 32, 128).transpose(2, 1, 0, 3)).reshape(32, 128, 4096)

    in_maps = []
    for c in range(NCORES):
        xT = np.zeros((NHALF, 128, NCH * TT), f)
        xTm = np.zeros((NHALF, 128, NCH * T), f)
        pos = np.zeros((128, NHALF * 16), f)
        for hf in range(NHALF):
            t0 = c * per_core + hf * T
            s0 = t0 % S
            blk = np.zeros((TT, D), f)
            blk[HALO:] = xf[t0:t0 + T]
            if s0 > 0:
                blk[:HALO] = xf[t0 - HALO:t0]
            xT[hf] = blk.T.reshape(NCH, 128, TT).transpose(1, 0, 2).reshape(128, NCH * TT)
            xTm[hf] = blk[HALO:].T.reshape(NCH, 128, T).transpose(1, 0, 2).reshape(128, NCH * T)
            pos[:, hf * 16:(hf + 1) * 16] = (s0 + 1 + np.arange(16, dtype=f))[None, :]
        in_maps.append({"xT": xT, "xTm": xTm, "prm": prm, "pos": pos, "win": win, "pw": pw, "wbr": wbr, "wo": wo})
    return in_maps, (B, S)


_NC_CACHE = {}


def kernel(x, norm_w, w_in, pool_w, pool_scale, conv_w, conv_b, gate_b, w_branch, w_out, final_norm_w):
    in_maps, (B, S) = _prep_inputs(x, norm_w, w_in, pool_w, pool_scale, conv_w, conv_b, gate_b, w_branch, w_out,
                                   final_norm_w)
    if "nc" not in _NC_CACHE:
        _NC_CACHE["nc"] = build_program()
    nc = _NC_CACHE["nc"]
    res = run_bass_kernel_spmd(nc, in_maps, core_ids=list(range(NCORES)))
    outs = []
    for c in range(NCORES):
        o = np.asarray(res.results[c]["outT"]).reshape(NHALF, 128, NCH, T)
        outs.append(o.transpose(0, 3, 2, 1).reshape(NHALF * T, D))
    return np.concatenate(outs, axis=0).reshape(B, S, D).astype(np.float32)
```
